# Optimizing a Trainium2 kernel written in Bass

```python
import jax
import jax.numpy as jnp
from jax import lax
import numpy as np

D_MODEL = 1024
BATCH = 8
SEQ = 4096
DEPTH = 1
DEC_BATCH = 8
DEC_SEQ = 8192
PAST_LEN = 128

GRID_W = 64
N_MEM = 256
D_FF = 2816
NORM_EPS = 1e-6

NA_HEADS = 8
NA_HD = 64
NA_WIN_R = 8
NA_WIN_C = 16
D_NA = NA_HEADS * NA_HD

RW_HEADS = 8
RW_HD = 64
D_RW = RW_HEADS * RW_HD
N_DIR = 2
DECAY_LORA = 64
AAA_LORA = 64
GATE_LORA = 160
RW_CONV = 3
GN_EPS = 64e-5

MEM_HEADS = 4
MEM_HD = 128
D_MEM = MEM_HEADS * MEM_HD

N_BRANCH = 3

C_NA = 3 * D_NA
C_RW = 3 * D_RW + N_DIR * DECAY_LORA + N_DIR * AAA_LORA + GATE_LORA
C_MQ = D_MEM
C_GATE = N_BRANCH * D_MODEL
N_IN = C_NA + C_RW + C_MQ + C_GATE
IN_SPLITS = (C_NA, C_NA + C_RW, C_NA + C_RW + C_MQ)
RW_SPLITS = (D_RW, 2 * D_RW, 3 * D_RW, 3 * D_RW + N_DIR * DECAY_LORA,
             3 * D_RW + N_DIR * (DECAY_LORA + AAA_LORA))

kernel_name = 'hybrid_natten_rwkv7_mem_encoder'


def rmsnorm(x, g, eps=NORM_EPS):
    xf = x.astype(jnp.float32)
    y = xf * lax.rsqrt(jnp.mean(xf * xf, axis=-1, keepdims=True) + eps)
    return (y * g.astype(jnp.float32)).astype(x.dtype)


def swiglu(h, w_gate, w_up, w_down):
    return (jax.nn.silu(h @ w_gate) * (h @ w_up)) @ w_down


def neighbourhood_attention(q, k, v, rpb):
    B, S, H, hd = q.shape
    rows = S // GRID_W
    wr = min(NA_WIN_R, rows)
    wc = NA_WIN_C
    qg = q.reshape(B, rows, GRID_W, H, hd)
    kg = k.reshape(B, rows, GRID_W, H, hd)
    vg = v.reshape(B, rows, GRID_W, H, hd)
    cols = np.arange(GRID_W)
    col_start = np.clip(cols - wc // 2, 0, GRID_W - wc)
    col_idx = col_start[:, None] + np.arange(wc)[None, :]
    dc = col_idx - cols[:, None]
    bias_c = rpb[:, :, dc + NA_WIN_C - 1]
    scale = hd ** -0.5

    def row_block(r):
        r0 = jnp.clip(r - wr // 2, 0, rows - wr)
        k_rows = lax.dynamic_slice_in_dim(kg, r0, wr, axis=1)
        v_rows = lax.dynamic_slice_in_dim(vg, r0, wr, axis=1)
        k_win = k_rows[:, :, col_idx]
        v_win = v_rows[:, :, col_idx]
        q_r = lax.dynamic_index_in_dim(qg, r, axis=1, keepdims=False)
        s = jnp.einsum('bchd,brcjhd->bhcrj', q_r, k_win).astype(jnp.float32) * scale
        dr = r0 + jnp.arange(wr) - r + NA_WIN_R - 1
        bias = jnp.take(bias_c, dr, axis=1).transpose(0, 2, 1, 3)
        s = s + bias[None].astype(jnp.float32)
        p = jax.nn.softmax(s.reshape(B, H, GRID_W, wr * wc), axis=-1)
        p = p.reshape(B, H, GRID_W, wr, wc).astype(v.dtype)
        return jnp.einsum('bhcrj,brcjhd->bchd', p, v_win)

    out = lax.map(row_block, jnp.arange(rows))
    return out.transpose(1, 0, 2, 3, 4).reshape(B, S, H * hd)


def centred_short_conv(x, w):
    C = x.shape[-1]
    K = w.shape[0]
    return lax.conv_general_dilated(
        x, w.astype(x.dtype)[:, None, :], window_strides=(1,),
        padding=[(K // 2, K // 2)], dimension_numbers=('NWC', 'WIO', 'NWC'),
        feature_group_count=C)


def rwkv7_bidirectional(z, conv_rw, w0_rw, w2_rw, a0_rw, a2_rw, g2_rw, k_k_rw, k_a_rw,
                        r_k_rw, ln_x_w_rw, ln_x_b_rw):
    B, S, _ = z.shape
    H, N = RW_HEADS, RW_HD
    f32 = jnp.float32
    z = centred_short_conv(z, conv_rw).astype(f32)
    r, k, v, xw, xa, xg = jnp.split(z, RW_SPLITS, axis=-1)
    xw = xw.reshape(B, S, N_DIR, DECAY_LORA)
    xa = xa.reshape(B, S, N_DIR, AAA_LORA)
    w = -jax.nn.softplus(-(w0_rw + jnp.einsum('bsdl,dlc->bsdc', jnp.tanh(xw), w2_rw))) - 0.5
    decay = jnp.exp(-jnp.exp(w))
    a = jax.nn.sigmoid(a0_rw + jnp.einsum('bsdl,dlc->bsdc', xa, a2_rw))
    g = jax.nn.sigmoid(xg) @ g2_rw
    kk = (k * k_k_rw).reshape(B, S, H, N)
    kk = kk / jnp.maximum(jnp.sqrt(jnp.sum(kk * kk, axis=-1, keepdims=True)), 1e-12)
    kd = k[:, :, None] * (1.0 + (a - 1.0) * k_a_rw)

    def heads(t):
        return t.reshape(t.shape[:-1] + (H, N))

    r, v = heads(r), heads(v)
    kd, decay, a = heads(kd), heads(decay), heads(a)

    def per_dir(t):
        t = jnp.stack([t[:, :, 0], jnp.flip(t[:, :, 1], axis=1)], axis=2)
        return t.transpose(1, 2, 0, 3, 4)

    def both(t):
        return per_dir(jnp.broadcast_to(t[:, :, None], (B, S, N_DIR, H, N)))

    def step(state, inp):
        r_t, w_t, k_t, v_t, kk_t, a_t = inp
        sa = jnp.einsum('dbhvk,dbhk->dbhv', state, -kk_t)
        state = (state * w_t[..., None, :]
                 + sa[..., :, None] * (kk_t * a_t)[..., None, :]
                 + v_t[..., :, None] * k_t[..., None, :])
        y = jnp.einsum('dbhvk,dbhk->dbhv', state, r_t)
        return state, y

    xs = (both(r), per_dir(decay), per_dir(kd), both(v), both(kk), per_dir(a))
    state0 = jnp.zeros((N_DIR, B, H, N, N), f32)
    _, ys = lax.scan(step, state0, xs)
    y = (ys[:, 0] + jnp.flip(ys[:, 1], axis=0)).transpose(1, 0, 2, 3)
    mu = jnp.mean(y, axis=-1, keepdims=True)
    var = jnp.mean(jnp.square(y - mu), axis=-1, keepdims=True)
    y = ((y - mu) * lax.rsqrt(var + GN_EPS)).reshape(B, S, D_RW) * ln_x_w_rw + ln_x_b_rw
    bonus = jnp.einsum('bshn,bsdhn,dhn->bsh', r, kd, r_k_rw)[..., None] * v
    y = y + bonus.reshape(B, S, D_RW)
    return y * g


def memory_attention(zq, mem, g_mem_norm, w_mem_kv, g_qn_mem, g_kn_mem):
    B, S, _ = zq.shape
    kv = rmsnorm(mem, g_mem_norm) @ w_mem_kv
    km, vm = jnp.split(kv, 2, axis=-1)
    M = mem.shape[1]
    q = rmsnorm(zq.reshape(B, S, MEM_HEADS, MEM_HD), g_qn_mem)
    km = rmsnorm(km.reshape(B, M, MEM_HEADS, MEM_HD), g_kn_mem)
    vm = vm.reshape(B, M, MEM_HEADS, MEM_HD)
    s = jnp.einsum('bshd,bmhd->bhsm', q, km).astype(jnp.float32) * (MEM_HD ** -0.5)
    p = jax.nn.softmax(s, axis=-1).astype(vm.dtype)
    return jnp.einsum('bhsm,bmhd->bshd', p, vm).reshape(B, S, D_MEM)


def encoder_layer(x, mem, p):
    B, S, _ = x.shape
    x = x + 0.5 * swiglu(rmsnorm(x, p['g_ffn1']), p['w_ffn1_gate'], p['w_ffn1_up'], p['w_ffn1_down'])
    h = rmsnorm(x, p['g_mix'])
    z = h @ p['w_in']
    z_na, z_rw, z_mq, z_gate = jnp.split(z, IN_SPLITS, axis=-1)
    qa, ka, va = jnp.split(z_na, 3, axis=-1)
    qa = rmsnorm(qa.reshape(B, S, NA_HEADS, NA_HD), p['g_qn_na'])
    ka = rmsnorm(ka.reshape(B, S, NA_HEADS, NA_HD), p['g_kn_na'])
    va = va.reshape(B, S, NA_HEADS, NA_HD)
    y_na = neighbourhood_attention(qa, ka, va, p['rpb_na']) @ p['w_o_na']
    y_rw = rwkv7_bidirectional(z_rw, p['conv_rw'], p['w0_rw'], p['w2_rw'], p['a0_rw'], p['a2_rw'],
                               p['g2_rw'], p['k_k_rw'], p['k_a_rw'], p['r_k_rw'],
                               p['ln_x_w_rw'], p['ln_x_b_rw']).astype(x.dtype) @ p['w_o_rw']
    y_mem = memory_attention(z_mq, mem, p['g_mem_norm'], p['w_mem_kv'],
                             p['g_qn_mem'], p['g_kn_mem']) @ p['w_o_mem']
    gates = jax.nn.sigmoid(z_gate).reshape(B, S, N_BRANCH, D_MODEL)
    merged = gates[:, :, 0] * y_na + gates[:, :, 1] * y_rw + gates[:, :, 2] * y_mem
    x = x + merged @ p['w_out']
    x = x + 0.5 * swiglu(rmsnorm(x, p['g_ffn2']), p['w_ffn2_gate'], p['w_ffn2_up'], p['w_ffn2_down'])
    return x


def setup_inputs(seed: int = 0):
    key = jax.random.key(seed)
    ks = iter(jax.random.split(key, 48))
    f32 = jnp.float32
    L = DEPTH

    def nrm(shape, scale):
        return scale * jax.random.normal(next(ks), shape, f32)

    def gain(shape):
        return 1.0 + 0.02 * jax.random.normal(next(ks), shape, f32)

    conv_base = jnp.array([0.25, 1.0, 0.25], f32)[None, :, None]
    return {
        'x_prompt': nrm((BATCH, SEQ, D_MODEL), 1.0),
        'x_sample': nrm((DEC_BATCH, DEC_SEQ, D_MODEL), 1.0),
        'mem_prompt': nrm((BATCH, N_MEM, D_MODEL), 1.0),
        'mem_sample': nrm((DEC_BATCH, N_MEM, D_MODEL), 1.0),
        'g_ffn1': gain((L, D_MODEL)),
        'w_ffn1_gate': nrm((L, D_MODEL, D_FF), D_MODEL ** -0.5),
        'w_ffn1_up': nrm((L, D_MODEL, D_FF), D_MODEL ** -0.5),
        'w_ffn1_down': nrm((L, D_FF, D_MODEL), D_FF ** -0.5),
        'g_mix': gain((L, D_MODEL)),
        'w_in': nrm((L, D_MODEL, N_IN), D_MODEL ** -0.5),
        'g_qn_na': gain((L, NA_HD)),
        'g_kn_na': gain((L, NA_HD)),
        'rpb_na': nrm((L, NA_HEADS, 2 * NA_WIN_R - 1, 2 * NA_WIN_C - 1), 0.02),
        'w_o_na': nrm((L, D_NA, D_MODEL), D_NA ** -0.5),
        'conv_rw': conv_base + nrm((L, RW_CONV, C_RW), 0.05),
        'w0_rw': jax.random.uniform(next(ks), (L, N_DIR, D_RW), f32, -6.0, 0.0),
        'w2_rw': nrm((L, N_DIR, DECAY_LORA, D_RW), 0.1 * DECAY_LORA ** -0.5),
        'a0_rw': nrm((L, N_DIR, D_RW), 0.1),
        'a2_rw': nrm((L, N_DIR, AAA_LORA, D_RW), 0.1 * AAA_LORA ** -0.5),
        'g2_rw': nrm((L, GATE_LORA, D_RW), GATE_LORA ** -0.5),
        'k_k_rw': 0.85 + nrm((L, D_RW), 0.02),
        'k_a_rw': gain((L, D_RW)),
        'r_k_rw': nrm((L, N_DIR, RW_HEADS, RW_HD), 0.1),
        'ln_x_w_rw': gain((L, D_RW)),
        'ln_x_b_rw': nrm((L, D_RW), 0.02),
        'w_o_rw': nrm((L, D_RW, D_MODEL), D_RW ** -0.5),
        'g_mem_norm': gain((L, D_MODEL)),
        'w_mem_kv': nrm((L, D_MODEL, 2 * D_MEM), D_MODEL ** -0.5),
        'g_qn_mem': gain((L, MEM_HD)),
        'g_kn_mem': gain((L, MEM_HD)),
        'w_o_mem': nrm((L, D_MEM, D_MODEL), D_MEM ** -0.5),
        'w_out': nrm((L, D_MODEL, D_MODEL), D_MODEL ** -0.5),
        'g_ffn2': gain((L, D_MODEL)),
        'w_ffn2_gate': nrm((L, D_MODEL, D_FF), D_MODEL ** -0.5),
        'w_ffn2_up': nrm((L, D_MODEL, D_FF), D_MODEL ** -0.5),
        'w_ffn2_down': nrm((L, D_FF, D_MODEL), D_FF ** -0.5),
    }


def reference(x_prompt, x_sample, mem_prompt, mem_sample,
              g_ffn1, w_ffn1_gate, w_ffn1_up, w_ffn1_down,
              g_mix, w_in,
              g_qn_na, g_kn_na, rpb_na, w_o_na,
              conv_rw, w0_rw, w2_rw, a0_rw, a2_rw, g2_rw, k_k_rw, k_a_rw, r_k_rw,
              ln_x_w_rw, ln_x_b_rw, w_o_rw,
              g_mem_norm, w_mem_kv, g_qn_mem, g_kn_mem, w_o_mem,
              w_out,
              g_ffn2, w_ffn2_gate, w_ffn2_up, w_ffn2_down):
    layer_params = {
        'g_ffn1': g_ffn1, 'w_ffn1_gate': w_ffn1_gate, 'w_ffn1_up': w_ffn1_up,
        'w_ffn1_down': w_ffn1_down, 'g_mix': g_mix, 'w_in': w_in,
        'g_qn_na': g_qn_na, 'g_kn_na': g_kn_na, 'rpb_na': rpb_na, 'w_o_na': w_o_na,
        'conv_rw': conv_rw, 'w0_rw': w0_rw, 'w2_rw': w2_rw, 'a0_rw': a0_rw, 'a2_rw': a2_rw,
        'g2_rw': g2_rw, 'k_k_rw': k_k_rw, 'k_a_rw': k_a_rw, 'r_k_rw': r_k_rw,
        'ln_x_w_rw': ln_x_w_rw, 'ln_x_b_rw': ln_x_b_rw, 'w_o_rw': w_o_rw,
        'g_mem_norm': g_mem_norm, 'w_mem_kv': w_mem_kv, 'g_qn_mem': g_qn_mem,
        'g_kn_mem': g_kn_mem, 'w_o_mem': w_o_mem, 'w_out': w_out,
        'g_ffn2': g_ffn2, 'w_ffn2_gate': w_ffn2_gate, 'w_ffn2_up': w_ffn2_up,
        'w_ffn2_down': w_ffn2_down,
    }
    y_prompt = x_prompt
    y_sample = x_sample
    for l in range(DEPTH):
        p = {name: arr[l] for name, arr in layer_params.items()}
        y_prompt = encoder_layer(y_prompt, mem_prompt, p)
        y_sample = encoder_layer(y_sample, mem_sample, p)
    return (y_prompt, y_sample)
```

```python
from contextlib import ExitStack
import numpy as np
import ml_dtypes
import concourse.bass as bass
import concourse.mybir as mybir
from concourse.bass_utils import run_bass_kernel_spmd

F32 = mybir.dt.float32
BF16 = mybir.dt.bfloat16
AF = mybir.ActivationFunctionType
ALU = mybir.AluOpType
AX = mybir.AxisListType
SEM_EPOCH = 20000
D = 1024
DFF = 2816
NFF = DFF // 128
NIN = 7072
EPS = 1e-6


class Buf:
    __slots__ = ("name", "last_w", "readers")

    def __init__(self, name):
        self.name = name
        self.last_w = None
        self.readers = {}


class Op:
    __slots__ = ("eng", "fn", "deps", "is_dma", "need_inc", "sem", "val", "dsem", "idx")

    def __init__(self, eng, fn, is_dma=False, dsem=None):
        self.eng = eng
        self.fn = fn
        self.deps = []
        self.is_dma = is_dma
        self.need_inc = False
        self.sem = None
        self.val = None
        self.dsem = dsem
        self.idx = None


class TV:
    __slots__ = ("ap", "name")

    def __init__(self, ap, name):
        self.ap = ap
        self.name = name

    def __getitem__(self, k):
        return TV(self.ap[k], self.name)

    def re(self, s, **kw):
        return TV(self.ap.rearrange(s, **kw), self.name)

    def bc(self, shape):
        return TV(self.ap.broadcast_to(shape), self.name)

    def us(self, ax):
        return TV(self.ap.unsqueeze(ax), self.name)

    def cast(self, dt):
        return TV(self.ap.bitcast(dt), self.name)


def _ap(x):
    return x.ap if isinstance(x, TV) else x


def _names(*xs):
    return [x.name for x in xs if isinstance(x, TV)]


class Prog:
    ENGS = ("pe", "act", "dve", "pool", "sp")

    def __init__(self, nc):
        self.nc = nc
        self.ops = {e: [] for e in self.ENGS}
        self.bufs = {}
        self.n_ops = 0
        self.last_dma = {}
        self.bar_fns = {}
        self.alias = {}

    def buf(self, name):
        b = self.bufs.get(name)
        if b is None:
            b = Buf(name)
            self.bufs[name] = b
        return b

    def add(self, eng, fn, reads=(), writes=(), dma=False, dsem=None):
        op = Op(eng, fn, is_dma=dma, dsem=dsem)
        op.idx = self.n_ops
        self.n_ops += 1
        deps = []
        if self.alias:
            reads = list(reads) + [x for b in reads for x in self.alias.get(b, ())]
            writes = list(writes) + [x for b in writes for x in self.alias.get(b, ())]
        reads = [self.buf(b) for b in reads]
        writes = [self.buf(b) for b in writes]
        for b in reads:
            if b.last_w is not None:
                deps.append((b.last_w, "raw"))
        for b in writes:
            if b.last_w is not None:
                deps.append((b.last_w, "waw"))
            for r in b.readers.values():
                deps.append((r, "war"))
        seen = set()
        for d, kind in deps:
            if d.is_dma:
                d = self.last_dma[d.dsem]
            if d is op or id(d) in seen:
                continue
            if (not d.is_dma) and (not dma) and d.eng == eng:
                if kind != "raw" or eng == "pe":
                    continue
            seen.add(id(d))
            op.deps.append(d)
            d.need_inc = True
        for b in reads:
            b.readers[("dma", op.idx) if dma else eng] = op
        for b in writes:
            b.last_w = op
            b.readers = {}
        self.ops[eng].append(op)
        if dma:
            self.last_dma[dsem] = op
        return op

    def mm(self, out, lhsT, rhs, start=True, stop=True):
        return self.add("pe", lambda e: e.matmul(out.ap, lhsT=lhsT.ap, rhs=rhs.ap, start=start, stop=stop),
                        reads=_names(lhsT, rhs), writes=[out.name])

    def tr(self, out, in_, ident):
        return self.add("pe", lambda e: e.transpose(out=out.ap, in_=in_.ap, identity=ident.ap),
                        reads=_names(in_, ident), writes=[out.name])

    def act(self, out, in_, func, scale=1.0, bias=None, accum=None, eng="act"):
        kw = {}
        if bias is not None:
            kw["bias"] = _ap(bias)
        if accum is not None:
            kw["accum_out"] = accum.ap
        w = [out.name] + ([accum.name] if accum is not None else [])
        return self.add(eng, lambda e: e.activation(out=out.ap, in_=in_.ap, func=func, scale=_ap(scale), **kw),
                        reads=_names(in_, scale, bias), writes=w)

    def tt(self, eng, out, a, b, op):
        return self.add(eng, lambda e: e.tensor_tensor(out=out.ap, in0=a.ap, in1=b.ap, op=op),
                        reads=_names(a, b), writes=[out.name])

    def ts(self, eng, out, a, s1, s2=None, op0=ALU.mult, op1=None):
        kw = {} if op1 is None else {"op1": op1}
        return self.add(eng, lambda e: e.tensor_scalar(out=out.ap, in0=a.ap, scalar1=_ap(s1), scalar2=_ap(s2),
                                                       op0=op0, **kw),
                        reads=_names(a, s1, s2), writes=[out.name])

    def stt(self, eng, out, a, s, b, op0, op1):
        return self.add(eng, lambda e: e.scalar_tensor_tensor(out=out.ap, in0=a.ap, scalar=_ap(s), in1=b.ap,
                                                              op0=op0, op1=op1),
                        reads=_names(a, s, b), writes=[out.name])

    def copy(self, eng, out, in_):
        if eng == "act":
            return self.act(out, in_, AF.Copy)
        return self.add(eng, lambda e: e.tensor_copy(out=out.ap, in_=in_.ap), reads=[in_.name], writes=[out.name])

    def memset(self, eng, out, val):
        return self.add(eng, lambda e: e.memset(out.ap, val), writes=[out.name])

    def reduce(self, eng, out, in_, op=ALU.add, axis=AX.X):
        return self.add(eng, lambda e: e.tensor_reduce(out=out.ap, in_=in_.ap, axis=axis, op=op),
                        reads=[in_.name], writes=[out.name])

    def recip(self, out, in_):
        return self.add("dve", lambda e: e.reciprocal(out=out.ap, in_=in_.ap), reads=[in_.name], writes=[out.name])

    def scan(self, out, d0, d1, init, op0, op1):
        return self.add("dve", lambda e: e.tensor_tensor_scan(out=out.ap, data0=d0.ap, data1=d1.ap, initial=_ap(init),
                                                              op0=op0, op1=op1),
                        reads=_names(d0, d1, init), writes=[out.name])

    def dma(self, out, in_, dsem, eng="sp"):
        return self.add(eng, lambda e: e.dma_start(out=out.ap, in_=in_.ap), reads=[in_.name], writes=[out.name],
                        dma=True, dsem=dsem)

    def barrier(self):
        r1 = []
        for e in ("pe", "act", "dve", "pool"):
            op = self.add(e, self.bar_fns[e], reads=["ident" if e == "pe" else "bs_" + e],
                          writes=["bs_" + e] + (["ps5", "psbig2"] if e == "pe" else []))
            op.need_inc = True
            r1.append(op)
        extra = r1 + list(self.last_dma.values())
        for e in self.ENGS:
            fn = self.bar_fns[e] if e != "sp" else (lambda eng: eng.nop())
            op = self.add(e, fn, reads=[] if e == "sp" else ["ident" if e == "pe" else "bs_" + e],
                          writes=["bs_" + e] + (["ps5", "psbig2"] if e == "pe" else []))
            have = set(id(d) for d in op.deps)
            for d in extra:
                if id(d) not in have and not (d.eng == e and not d.is_dma):
                    op.deps.append(d)
                    d.need_inc = True
        keep = {n: self.bufs[n] for n in ("ident", "bs_act", "bs_dve", "bs_pool", "bs_pe", "bs_sp") if n in self.bufs}
        self.bufs = keep

    def finalize(self, stack):
        nc = self.nc
        sem_cache = {}

        def get_sem(key):
            s = sem_cache.get(key)
            if s is None:
                s = stack.enter_context(nc.semaphore("s%d" % len(sem_cache)))
                sem_cache[key] = s
            return s

        dma_counts = {}
        final_tokens = {}
        all_ops = sorted((op for e in self.ENGS for op in self.ops[e]), key=lambda o: o.idx)
        for op in all_ops:
            if op.is_dma:
                key = ("d", op.dsem)
                c = dma_counts.get(key, 0) + 16
                dma_counts[key] = c
                op.sem = key
                op.val = c
                final_tokens[key] = c
        for e in self.ENGS:
            cnt = 0
            epoch = 0
            for op in self.ops[e]:
                if (not op.is_dma) and op.need_inc:
                    cnt += 1
                    if cnt > SEM_EPOCH:
                        epoch += 1
                        cnt = 1
                    op.sem = ("e", e, epoch)
                    op.val = cnt
        for op in all_ops:
            if op.sem is not None:
                get_sem(op.sem)
        block = stack.enter_context(nc.Block())
        engmap = {"pe": block.tensor, "act": block.scalar, "dve": block.vector,
                  "pool": block.gpsimd, "sp": block.sync}
        self.n_waits = 0

        def make_stream(e):
            ops = self.ops[e]

            def stream(eng):
                waited = {}
                for op in ops:
                    need = {}
                    for d in op.deps:
                        k = d.sem
                        if k[0] == "e":
                            cur = waited.get(k[1], (-1, 0))
                            if (k[2], d.val) <= cur:
                                continue
                        elif waited.get(k, 0) >= d.val:
                            continue
                        prev = need.get(k)
                        if prev is None or d.val > prev:
                            need[k] = d.val
                    for k, v in sorted(need.items(), key=lambda kv: str(kv[0])):
                        eng.wait_ge(sem_cache[k], v)
                        self.n_waits += 1
                        if k[0] == "e":
                            cur = waited.get(k[1], (-1, 0))
                            if (k[2], v) > cur:
                                waited[k[1]] = (k[2], v)
                        else:
                            waited[k] = v
                    ins = op.fn(eng)
                    if op.is_dma:
                        ins.then_inc(sem_cache[op.sem], 16)
                    elif op.need_inc and op.sem is not None:
                        ins.then_inc(sem_cache[op.sem], 1)
                if e == "sp":
                    for k, v in final_tokens.items():
                        eng.wait_ge(sem_cache[k], v)
            return stream

        for e in self.ENGS:
            engmap[e](make_stream(e))


class Arena:
    def __init__(self, base_ap, nbytes):
        self.base = base_ap
        self.nbytes = nbytes
        self.off = 0
        self.gen = 0

    def reset(self):
        self.off = 0
        self.gen += 1

    def alloc(self, name, shape, dt=BF16):
        esz = 4 if dt == F32 else 2
        n = 1
        for s in shape[1:]:
            n *= s
        size = (n * esz + 63) // 64 * 64
        a = self.off
        self.off += size
        assert self.off <= self.nbytes, "arena overflow %s %d" % (name, self.off)
        v = self.base[0:shape[0], a // 2:(a + n * esz) // 2]
        if dt != BF16:
            v = v.bitcast(dt)
        if len(shape) == 3:
            v = v.rearrange("p (a b) -> p a b", a=shape[1])
        elif len(shape) == 4:
            v = v.rearrange("p (a b c) -> p a b c", a=shape[1], b=shape[2])
        return TV(v, "%s#%d" % (name, self.gen))


class K:
    pass


def rms_transpose(P, k, x_tv, out_hT, junk, stat, xn, ptr, nfeat=D):
    nchunk = nfeat // 128
    P.memset("dve", stat[:, 0:1], 0.0)
    P.act(junk, x_tv, AF.Square, accum=stat[:, 0:1])
    P.ts("dve", stat[:, 1:2], stat[:, 0:1], 1.0 / nfeat, EPS, op0=ALU.mult, op1=ALU.add)
    P.act(stat[:, 2:3], stat[:, 1:2], AF.Ln)
    P.act(stat[:, 3:4], stat[:, 2:3], AF.Exp, scale=-0.5)
    P.ts("dve", xn, x_tv, stat[:, 3:4], None, op0=ALU.mult)
    for c in range(nchunk):
        P.tr(ptr[:, c * 128:(c + 1) * 128], xn[:, c * 128:(c + 1) * 128], k.ident)
    P.copy("act", out_hT, ptr[:, 0:nfeat].re("p (c t) -> p c t", c=nchunk))


def load_weight_cast(P, dst, src_ap, nrow_chunks, name, dsem):
    for c in range(nrow_chunks):
        P.dma(dst[:, c, :], TV(src_ap[c * 128:(c + 1) * 128, :], name), dsem, eng="pool")


def ffn_phase(P, k, tag, src, dst, T, wg, wu, wd, gcol, h2T_dram=None):
    A = k.arena
    A.reset()
    G = 256
    ng = T // G
    WG = A.alloc("WG", [128, 8, DFF])
    WU = A.alloc("WU", [128, 8, DFF])
    WD = A.alloc("WD", [128, NFF, D])
    gc = A.alloc("gc", [128, 8], F32)
    xs = [A.alloc("x%d" % i, [128, 2, D], F32) for i in range(2)]
    hT = A.alloc("hT", [128, 8, G])
    actT = A.alloc("actT", [128, NFF, G])
    sg = [A.alloc("sg%d" % i, [128, G], F32) for i in range(2)]
    junk = A.alloc("junk", [128, D], F32)
    xn = A.alloc("xn", [128, D])
    stat = A.alloc("stat", [128, 8], F32)
    h2s = [A.alloc("h2s%d" % i, [128, 8, G]) for i in range(2)] if h2T_dram is not None else None
    P.dma(gc, TV(gcol, "gcol"), "misc")
    load_weight_cast(P, WG, wg, 8, "wg", "wload")
    load_weight_cast(P, WU, wu, 8, "wu", "wload")
    load_weight_cast(P, WD, wd, NFF, "wd", "wload")
    for c in range(8):
        P.ts("dve", WG[:, c, :], WG[:, c, :], gc[:, c:c + 1], None, op0=ALU.mult)
        P.ts("dve", WU[:, c, :], WU[:, c, :], gc[:, c:c + 1], None, op0=ALU.mult)
    psb = k.psb
    ps = k.ps
    for g in range(ng):
        xg = xs[g % 2]
        t0 = g * G
        P.dma(xg, TV(src[t0:t0 + G, :].rearrange("(i p) d -> p i d", p=128), "%s_src%d" % (tag, g)), "xin%d" % (g % 2))
        for i in range(2):
            rms_transpose(P, k, xg[:, i, :], hT[:, :, i * 128:(i + 1) * 128], junk, stat, xn, psb[i % 2])
        for f in range(NFF):
            pg = ps[(2 * f) % 4]
            pu = ps[(2 * f + 1) % 4]
            for c in range(8):
                P.mm(pg[:, 0:G], WG[:, c, f * 128:(f + 1) * 128], hT[:, c, :], start=(c == 0), stop=(c == 7))
            for c in range(8):
                P.mm(pu[:, 0:G], WU[:, c, f * 128:(f + 1) * 128], hT[:, c, :], start=(c == 0), stop=(c == 7))
            s = sg[f % 2]
            P.act(s, pg[:, 0:G], AF.Silu)
            P.tt("dve", actT[:, f, :], s, pu[:, 0:G], ALU.mult)
        for i in range(2):
            for n in range(2):
                pd = ps[4 + (2 * i + n) % 2]
                for f in range(NFF):
                    P.mm(pd, actT[:, f, i * 128:(i + 1) * 128], WD[:, f, n * 512:(n + 1) * 512],
                         start=(f == 0), stop=(f == NFF - 1))
                P.stt("dve", xg[:, i, n * 512:(n + 1) * 512], pd, 0.5, xg[:, i, n * 512:(n + 1) * 512],
                      ALU.mult, ALU.add)
        P.dma(TV(dst[t0:t0 + G, :].rearrange("(i p) d -> p i d", p=128), "%s_dst%d" % (tag, g)), xg, "xout%d" % (g % 2))
        if h2T_dram is not None:
            h2 = h2s[g % 2]
            for i in range(2):
                rms_transpose(P, k, xg[:, i, :], h2[:, :, i * 128:(i + 1) * 128], junk, stat, xn, psb[i % 2])
            P.dma(TV(h2T_dram[:, t0:t0 + G].rearrange("(c p) t -> p c t", p=128), "h2T%d" % g), h2, "h2out%d" % (g % 2))
    P.barrier()


C_Q, C_K, C_V, C_RW, C_MQ, C_GATE = 0, 512, 1024, 1536, 3488, 4000
NRW = 1952


def head_norm(P, k, zsrc, nheads, hd, gvec, out_bf, sq, stat, scale_fix):
    n = nheads * hd
    P.act(sq[:, 0:n], zsrc, AF.Square)
    P.reduce("dve", stat[:, 0:nheads], sq[:, 0:n].re("p (h d) -> p h d", h=nheads))
    P.ts("dve", stat[:, 32:32 + nheads], stat[:, 0:nheads], 1.0 / hd, EPS, op0=ALU.mult, op1=ALU.add)
    P.act(stat[:, 64:64 + nheads], stat[:, 32:32 + nheads], AF.Ln)
    P.act(stat[:, 96:96 + nheads], stat[:, 64:64 + nheads], AF.Exp, scale=-0.5)
    for (h0, h1, f) in scale_fix:
        P.ts("dve", stat[:, 96 + h0:96 + h1], stat[:, 96 + h0:96 + h1], f, None, op0=ALU.mult)
    P.tt("dve", sq[:, 0:n].re("p (h d) -> p h d", h=nheads), zsrc.re("p (h d) -> p h d", h=nheads),
         stat[:, 96:96 + nheads].us(2).bc([128, nheads, hd]), ALU.mult)
    P.tt("dve", out_bf, sq[:, 0:n], gvec, ALU.mult)


def memkv_phase(P, k, seqs, mem, wkv, gmem_c, gvec_km):
    A = k.arena
    A.reset()
    WKV = A.alloc("WKV", [128, 8, D])
    gc = A.alloc("gc", [128, 8], F32)
    gkm = A.alloc("gkm", [128, 512], F32)
    mt = A.alloc("mt", [128, D], F32)
    junk = A.alloc("junk", [128, D], F32)
    xn = A.alloc("xn", [128, D])
    stat = A.alloc("stat", [128, 128], F32)
    mT = A.alloc("mT", [128, 8, 128])
    zk = A.alloc("zk", [128, 512], F32)
    sq = A.alloc("sq", [128, 512], F32)
    kn = A.alloc("kn", [128, 512])
    P.dma(gc, TV(gmem_c, "gmem_c"), "misc")
    P.dma(gkm, TV(gvec_km.partition_broadcast(128), "gvec_km"), "misc")
    load_weight_cast(P, WKV, wkv, 8, "wkv", "wload")
    for c in range(8):
        P.ts("dve", WKV[:, c, :], WKV[:, c, :], gc[:, c:c + 1], None, op0=ALU.mult)
    for s in range(len(seqs)):
        for i in range(2):
            P.dma(mt, TV(mem[s * 256 + i * 128:s * 256 + (i + 1) * 128, :], "mem"), "xin0")
            rms_transpose(P, k, mt, mT, junk, stat, xn, k.psb[0])
            for n in range(2):
                pz = k.ps[n]
                for c in range(8):
                    P.mm(pz, mT[:, c, :], WKV[:, c, n * 512:(n + 1) * 512], start=(c == 0), stop=(c == 7))
            P.copy("act", zk, k.ps[0])
            P.copy("dve", k.vm[s][:, i, :], k.ps[1])
            head_norm(P, k, zk, 4, 128, gkm, kn, sq, stat, [])
            for h in range(4):
                P.tr(k.psb[1][:, h * 128:(h + 1) * 128], kn[:, h * 128:(h + 1) * 128], k.ident)
            P.copy("act", k.kmT[s][:, :, i * 128:(i + 1) * 128], k.psb[1][:, 0:512].re("p (h t) -> p h t", h=4))
    P.barrier()


def proj_phase(P, k, seqs, h2T, w_in, gmix_c, gvec_q, qT, kT, vna, zrw, ymT_d, gatesT):
    A = k.arena
    A.reset()
    G = 512
    WIN = A.alloc("WIN", [128, 8, NIN])
    gc = A.alloc("gc", [128, 8], F32)
    gq = A.alloc("gq", [128, 1536], F32)
    hin = [A.alloc("hin%d" % i, [128, 8, G]) for i in range(2)]
    zq = A.alloc("zq", [128, 1536], F32)
    sq = A.alloc("sq", [128, 1536], F32)
    qn = A.alloc("qn", [128, 1536])
    stat = A.alloc("stat", [128, 128], F32)
    qkT = A.alloc("qkT", [128, 8, G])
    mqT = A.alloc("mqT", [128, 4, G])
    vst = A.alloc("vst", [128, 4, 512])
    PT = [A.alloc("PT%d" % i, [128, 2, G]) for i in range(2)]
    rc = A.alloc("rc", [128, G], F32)
    ymT = A.alloc("ymT", [128, 4, G])
    stg = [A.alloc("stg%d" % i, [128, 4, G]) for i in range(2)]
    P.dma(gc, TV(gmix_c, "gmix_c"), "misc")
    P.dma(gq, TV(gvec_q.partition_broadcast(128), "gvec_q"), "misc")
    load_weight_cast(P, WIN, w_in, 8, "w_in", "wload")
    for c in range(8):
        P.ts("dve", WIN[:, c, :], WIN[:, c, :], gc[:, c:c + 1], None, op0=ALU.mult)
    ps = k.ps
    gi = 0
    t0 = 0
    for s, S in enumerate(seqs):
        for gg in range(S // G):
            h = hin[gi % 2]
            P.dma(h, TV(h2T[:, t0:t0 + G].rearrange("(c p) t -> p c t", p=128), "h2T_g%d" % gi), "xin%d" % (gi % 2))
            for i in range(4):
                lt = lambda c: h[:, c, i * 128:(i + 1) * 128]
                for n, c0 in enumerate((C_Q, C_K, C_MQ, C_V)):
                    for c in range(8):
                        P.mm(ps[n], lt(c), WIN[:, c, c0:c0 + 512], start=(c == 0), stop=(c == 7))
                P.copy("act", zq[:, 0:512], ps[0])
                P.copy("act", zq[:, 512:1024], ps[1])
                P.copy("act", zq[:, 1024:1536], ps[2])
                P.copy("dve", vst[:, i, :], ps[3])
                head_norm(P, k, zq[:, 0:1024], 16, 64, gq[:, 0:1024], qn[:, 0:1024], sq, stat, [(0, 8, 0.125)])
                head_norm(P, k, zq[:, 1024:1536], 4, 128, gq[:, 1024:1536], qn[:, 1024:1536], sq, stat,
                          [(0, 4, 128.0 ** -0.5)])
                for c in range(8):
                    P.tr(k.psb[0][:, c * 128:(c + 1) * 128], qn[:, c * 128:(c + 1) * 128], k.ident)
                for c in range(4):
                    P.tr(k.psb[1][:, c * 128:(c + 1) * 128], qn[:, 1024 + c * 128:1024 + (c + 1) * 128], k.ident)
                P.copy("act", qkT[:, :, i * 128:(i + 1) * 128], k.psb[0][:, 0:1024].re("p (c t) -> p c t", c=8))
                P.copy("dve", mqT[:, :, i * 128:(i + 1) * 128], k.psb[1][:, 0:512].re("p (c t) -> p c t", c=4))
            P.dma(TV(qT[:, t0:t0 + G].rearrange("(c p) t -> p c t", p=128), "qT_g%d" % gi), qkT[:, 0:4, :], "qkout")
            P.dma(TV(kT[:, t0:t0 + G].rearrange("(c p) t -> p c t", p=128), "kT_g%d" % gi), qkT[:, 4:8, :], "qkout")
            P.dma(TV(vna[t0:t0 + G, :].rearrange("(i p) c -> p i c", p=128), "vna_g%d" % gi), vst, "vout")
            for hh in range(4):
                pt = PT[hh % 2]
                for mc in range(2):
                    P.mm(ps[4 + mc], k.kmT[s][:, hh, mc * 128:(mc + 1) * 128], mqT[:, hh, :])
                    P.act(pt[:, mc, :], ps[4 + mc], AF.Exp)
                for mc in range(2):
                    P.mm(ps[0], k.vm[s][:, mc, hh * 128:(hh + 1) * 128], pt[:, mc, :], start=(mc == 0), stop=(mc == 1))
                for mc in range(2):
                    P.mm(ps[1], k.ones, pt[:, mc, :], start=(mc == 0), stop=(mc == 1))
                P.recip(rc, ps[1])
                P.tt("dve", ymT[:, hh, :], ps[0], rc, ALU.mult)
            P.dma(TV(ymT_d[:, t0:t0 + G].rearrange("(c p) t -> p c t", p=128), "ymT_g%d" % gi), ymT, "ymout")
            chunks = [(C_RW + j * 128, min(128, NRW - j * 128), "rw", j) for j in range(16)]
            chunks += [(C_GATE + j * 128, 128, "gate", j) for j in range(24)]
            for ci, (c0, wd_, kind, j) in enumerate(chunks):
                pz = ps[2 + ci % 4]
                for c in range(8):
                    P.mm(pz[0:wd_, :], WIN[:, c, c0:c0 + wd_], h[:, c, :], start=(c == 0), stop=(c == 7))
                sg_ = stg[(ci // 4) % 2]
                if kind == "rw":
                    P.copy("dve", sg_[0:wd_, j % 4, :], pz[0:wd_, :])
                else:
                    P.act(sg_[0:wd_, j % 4, :], pz[0:wd_, :], AF.Sigmoid)
                if ci % 4 == 3:
                    j0 = j - 3
                    if kind == "rw":
                        if j == 15:
                            P.dma(TV(zrw[j0 * 128:j0 * 128 + 384, t0:t0 + G].rearrange("(c p) t -> p c t", p=128),
                                     "zrw_g%d" % gi), sg_[:, 0:3, :], "stg%d" % ((ci // 4) % 2))
                            P.dma(TV(zrw[1920:1952, t0:t0 + G], "zrw_g%d" % gi), sg_[0:32, 3, :], "stg%d" % ((ci // 4) % 2))
                        else:
                            P.dma(TV(zrw[j0 * 128:j0 * 128 + 512, t0:t0 + G].rearrange("(c p) t -> p c t", p=128),
                                     "zrw_g%d" % gi), sg_, "stg%d" % ((ci // 4) % 2))
                    else:
                        P.dma(TV(gatesT[j0 * 128:j0 * 128 + 512, t0:t0 + G].rearrange("(c p) t -> p c t", p=128),
                                 "gates_g%d" % gi), sg_, "stg%d" % ((ci // 4) % 2))
            gi += 1
            t0 += G
    P.barrier()


GN_EPS = 64e-5


def rwkv_phase(P, k, seqs, zrw, y0_d, yrwT_d, w):
    A = k.arena
    A.reset()
    BLK = 512
    NC = BLK // 64
    ps = k.ps
    W2 = A.alloc("W2", [128, 512])
    A2 = A.alloc("A2", [128, 512])
    G2a = A.alloc("G2a", [128, 512])
    G2b = A.alloc("G2b", [32, 512])
    cw = A.alloc("cw", [128, 16, 3], F32)
    pc = A.alloc("pc", [128, 32], F32)
    nw0 = A.alloc("nw0", [128, 8], F32)
    kc = A.alloc("kc", [128, 8], F32)
    rmask = A.alloc("rmask", [128, BLK], F32)
    idst = A.alloc("idst", [128, 64], F32)
    bones = A.alloc("bones", [128, 128])
    m1 = A.alloc("m1", [64, 2, 128], F32)
    m2 = A.alloc("m2", [64, 2, 64], F32)
    lnw = A.alloc("lnw", [64, 512], F32)
    lnb = A.alloc("lnb", [64, 512], F32)
    for dst, src, nm in ((W2, w["w2_rw"], "w2"), (A2, w["a2_rw"], "a2"), (G2a, w["g2_rw"][0:128, :], "g2a"),
                         (G2b, w["g2_rw"][128:160, :], "g2b")):
        P.dma(dst, TV(src, nm), "wload", eng="pool")
    P.dma(cw, TV(w["conv_c"], "conv_c"), "misc")
    P.dma(pc, TV(w["rw_pc"], "rw_pc"), "misc")
    P.dma(rmask, TV(w["rmask"], "rmask"), "misc")
    P.dma(idst, TV(w["idst"], "idst"), "misc")
    P.dma(bones, TV(w["bones"], "bones"), "misc")
    P.dma(m1, TV(w["m1"], "m1"), "misc")
    P.dma(m2, TV(w["m2"], "m2"), "misc")
    P.dma(lnw, TV(w["ln_w"].partition_broadcast(64), "ln_w"), "misc")
    P.dma(lnb, TV(w["ln_b"].partition_broadcast(64), "ln_b"), "misc")
    P.ts("dve", nw0, pc[:, 0:8], -1.0, None, op0=ALU.mult)
    P.memset("pool", kc[:, 0:1], 1.0)
    P.memset("pool", kc[:, 1:2], -0.5)
    P.memset("pool", kc[:, 2:3], GN_EPS)
    zin = [A.alloc("zin0", [128, 16, BLK + 2])] * 2
    zc = A.alloc("zc", [128, 16, BLK], F32)
    tw = A.alloc("tw", [128, BLK])
    xab = A.alloc("xab", [128, BLK])
    sgg = A.alloc("sgg", [128, BLK])
    sgg2 = A.alloc("sgg2", [32, BLK])
    f = {n: A.alloc(n, [128, BLK], F32) for n in ("ew", "a", "kk", "kd", "beta", "CE", "d1", "ex1", "ex2", "tmp")}
    f["e1"] = f["tmp"]
    f["t1"] = f["tmp"]
    f["sp"] = f["d1"]
    f["kkr"] = f["kk"]
    f["ssk"] = f["a"]
    f["ea"] = f["ex1"]
    f["em"] = f["ex1"]
    f["ep"] = f["ex2"]
    f["eh"] = f["ex2"]
    sqk = A.alloc("sqk", [128, BLK])
    bsb = sqk
    TE = A.alloc("TE", [128, NC], F32)
    wtot = A.alloc("wtot", [128, NC], F32)
    AR = [A.alloc("AR%d" % j, [128, NC, 2, 64]) for j in range(4)]
    BK = [A.alloc("BK%d" % j, [128, NC, 2, 64]) for j in range(4)]
    BKh = [A.alloc("BKh%d" % j, [128, NC, 2, 64]) for j in range(4)]
    VT = [A.alloc("VT%d" % j, [128, BLK + 64]) for j in range(4)]
    DW = [A.alloc("DW%d" % j, [128, NC, 64]) for j in range(4)]
    gT = A.alloc("gT", [128, 4, BLK])
    cb = A.alloc("cb", [128, 4, BLK])
    ynT = A.alloc("ynT", [128, 4, BLK])
    yo = A.alloc("yo", [128, 4, BLK])
    ATb = A.alloc("ATb", [64, 8, 128])
    ATk = A.alloc("ATk", [64, 8, 128])
    Nm = A.alloc("Nm", [64, 8, 64])
    Atm = A.alloc("Atm", [64, 8, 64])
    Bhtm = A.alloc("Bhtm", [64, 8, 64])
    Khtm = A.alloc("Khtm", [64, 8, 64])
    Vtm = A.alloc("Vtm", [64, 8, 64])
    X = A.alloc("X", [64, 8, 128])
    QQ = [A.alloc("QQ%d" % i, [64, 8, 128]) for i in range(2)]
    GE = A.alloc("GE", [64, 8, 128])
    Tst = A.alloc("Tst", [64, 8, 64])
    GEadd = A.alloc("GEadd", [64, NC, 8, 128])
    y0s = A.alloc("y0s", [64, NC, 512], F32)
    yt = A.alloc("yt", [64, 512], F32)
    yc = A.alloc("yc", [64, 512], F32)
    ysq = yt
    ynb = A.alloc("ynb", [64, 512])
    gst = A.alloc("gst", [64, 64], F32)
    for j in range(4):
        P.memset("pool", VT[j][:, BLK:BLK + 64], 0.0)
    ZA = k.psbig[0]
    ZB = k.psbig[1]
    ZC = k.psbig[2]
    za = lambda Z: Z[0:64, :].re("p (h c) -> p h c", h=8)
    idb = k.ident
    ps = [ZA[:, 0:512], ZA[:, 512:1024], ZB[:, 0:512], ZB[:, 512:1024]]

    def prep_pair(j, d, c, want_bonus_dirs, want_g):
        r_j, k_j, v_j = zc[:, j, :], zc[:, 4 + j, :], zc[:, 8 + j, :]
        hs = slice(64 * d, 64 * d + 64)
        P.mm(ps[0], W2[hs, j * 128:(j + 1) * 128], tw[hs, :])
        P.act(f["e1"], ps[0], AF.Exp, scale=-1.0, bias=nw0[:, d * 4 + j:d * 4 + j + 1])
        P.act(f["sp"], f["e1"], AF.Ln, bias=kc[:, 0:1])
        P.act(f["ew"], f["sp"], AF.Exp, scale=-1.0, bias=kc[:, 1:2])
        P.ts("dve", f["kkr"], k_j, pc[:, 16 + j:17 + j], None, op0=ALU.mult)
        P.act(sqk, f["kkr"], AF.Square)
        P.mm(ps[2], bones, sqk)
        P.ts("dve", f["ssk"], ps[2], 1e-24, None, op0=ALU.max)
        P.act(f["ssk"], f["ssk"], AF.Ln)
        P.act(f["ssk"], f["ssk"], AF.Exp, scale=-0.5)
        P.tt("dve", f["kk"], f["kkr"], f["ssk"], ALU.mult)
        first = True
        for dd in want_bonus_dirs:
            hs2 = slice(64 * dd, 64 * dd + 64)
            P.mm(ps[1], A2[hs2, j * 128:(j + 1) * 128], xab[hs2, :])
            P.act(f["a"], ps[1], AF.Sigmoid, bias=pc[:, 8 + dd * 4 + j:9 + dd * 4 + j])
            P.ts("dve", f["t1"], f["a"], -1.0, pc[:, 20 + j:21 + j], op0=ALU.add, op1=ALU.mult)
            P.stt("dve", f["kd"], f["t1"], 1.0, k_j, ALU.add, ALU.mult)
            if len(want_bonus_dirs) > 1:
                P.stt("dve", bsb, r_j, pc[:, 24 + dd * 4 + j:25 + dd * 4 + j], f["kd"], ALU.mult, ALU.mult)
                P.mm(ps[3], bones, bsb, start=first, stop=(dd == want_bonus_dirs[-1]))
                first = False
        if len(want_bonus_dirs) > 1:
            P.tt("dve", cb[:, j, :], ps[3], v_j, ALU.mult)
        if want_g:
            P.mm(ps[1], G2a[:, j * 128:(j + 1) * 128], sgg, start=True, stop=False)
            P.mm(ps[1], G2b[:, j * 128:(j + 1) * 128], sgg2, start=False, stop=True)
            P.copy("act", gT[:, j, :], ps[1])
        P.tt("dve", f["beta"], f["kk"], f["a"], ALU.mult)
        P.scan(f["CE"], rmask, f["ew"], 0.0, ALU.mult, ALU.add)
        ce3 = f["CE"].re("p (c t) -> p c t", c=NC)
        P.copy("dve", TE, ce3[:, :, 63])
        if d == 1:
            P.tt("dve", f["tmp"].re("p (c t) -> p c t", c=NC), TE.us(2).bc([128, NC, 64]), ce3, ALU.subtract)
            P.tt("dve", f["CE"], f["tmp"], f["ew"], ALU.add)
        P.act(wtot, TE, AF.Exp, scale=-1.0)
        v3 = lambda t_: t_.re("p (c t) -> p c t", c=NC)
        P.tt("dve", f["d1"], f["ew"], f["CE"], ALU.subtract)
        P.act(f["ea"], f["d1"], AF.Exp)
        P.act(f["ep"], f["CE"], AF.Exp)
        P.stt("dve", AR[j][:, :, 0, :], v3(f["kk"]), -1.0, v3(f["ea"]), ALU.mult, ALU.mult)
        P.act(f["em"], f["CE"], AF.Exp, scale=-1.0)
        P.tt("dve", BK[j][:, :, 0, :], v3(f["beta"]), v3(f["ep"]), ALU.mult)
        P.tt("dve", BK[j][:, :, 1, :], v3(f["kd"]), v3(f["ep"]), ALU.mult)
        P.tt("dve", f["tmp"].re("p (c t) -> p c t", c=NC), ce3, TE.us(2).bc([128, NC, 64]), ALU.subtract)
        P.act(f["eh"], f["tmp"], AF.Exp)
        P.tt("dve", AR[j][:, :, 1, :], v3(r_j), v3(f["em"]), ALU.mult)
        P.tt("dve", BKh[j][:, :, 0, :], v3(f["beta"]), v3(f["eh"]), ALU.mult)
        P.tt("dve", BKh[j][:, :, 1, :], v3(f["kd"]), v3(f["eh"]), ALU.mult)
        P.copy("act", VT[j][:, 0:BLK], v_j)
        P.tt("dve", DW[j], idst.us(1).bc([128, NC, 64]), wtot.us(2).bc([128, NC, 64]), ALU.mult)
        P.copy("act", GEadd[:, :, j, 0:64], DW[j][0:64, :, :])
        P.copy("act", GEadd[:, :, j, 64:128], AR[j][0:64, :, 1, :])
        P.mm(ps[2][0:64, :], idb[64:128, 64:128], DW[j][64:128, :, :].re("p c k -> p (c k)"))
        P.mm(ps[3][0:64, :], idb[64:128, 64:128], AR[j][64:128, :, 1, :])
        P.copy("act", GEadd[:, :, 4 + j, 0:64], ps[2][0:64, :].re("p (c k) -> p c k", c=NC))
        P.copy("act", GEadd[:, :, 4 + j, 64:128], ps[3][0:64, :].re("p (c k) -> p c k", c=NC))

    NP = 2
    SP = 8 // NP
    for Z_ in (ZA, ZB, ZC):
        P.alias[Z_.name] = [Z_.name + "/%d" % p_ for p_ in range(NP)]

    def hv(t_, p):
        return TV(t_.ap[:, SP * p:SP * p + SP], t_.name + "/%d" % p)

    def zh(Z, p):
        return TV(Z.ap[0:64, p * SP * 128:(p + 1) * SP * 128].rearrange("p (h c) -> p h c", h=SP), Z.name + "/%d" % p)

    def unit_chunk(c, d, t_glob, last_sweep):
        PARTS = range(NP)
        m1b = m1[:, d, :].us(1).bc([64, SP, 128])
        m2b = m2[:, d, :].us(1).bc([64, SP, 64])
        pa, pb = k.psb[0], k.psb[1]
        for j in range(4):
            P.tr(pa[0:64, j * 128:(j + 1) * 128], AR[j][:, c, 0, :], idb)
            P.tr(pa[0:64, 512 + j * 128:512 + (j + 1) * 128], BKh[j][:, c, 0, :], idb)
            P.tr(pb[0:64, j * 128:(j + 1) * 128], BKh[j][:, c, 1, :], idb)
            P.tr(pb[:, 512 + j * 128:512 + (j + 1) * 128], VT[j][:, c * 64:c * 64 + 128], idb)
        so = lambda t_: t_.re("p (e j) c -> p j e c", e=2)
        si = lambda t_: t_.re("p (j e c) -> p j e c", j=4, e=2)
        P.copy("act", so(Atm), si(pa[0:64, 0:512]))
        P.copy("act", so(Bhtm), si(pa[0:64, 512:1024]))
        P.copy("act", so(Khtm), si(pb[0:64, 0:512]))
        P.copy("act", so(Vtm), si(pb[0:64, 512:1024]))
        for p in PARTS:
            za_, zb_, zc_ = zh(ZA, p), zh(ZB, p), zh(ZC, p)
            for jj in range(SP):
                sl = SP * p + jj
                e, j = sl // 4, sl % 4
                hs = slice(64 * e, 64 * e + 64)
                ar = AR[j][hs, c, :, :].re("p a t -> p (a t)")
                P.mm(za_[:, jj, :], BK[j][hs, c, 0, :], ar)
                P.mm(zb_[:, jj, :], BK[j][hs, c, 1, :], ar)
                P.mm(zc_[:, jj, 0:64], AR[j][hs, c, 0, :], BK[j][hs, c, 0, :])
        for p in PARTS:
            P.tt("dve", hv(ATb, p), zh(ZA, p), m1b, ALU.mult)
            P.tt("dve", hv(ATk, p), zh(ZB, p), m1b, ALU.mult)
            P.tt("dve", hv(Nm, p), zh(ZC, p)[:, :, 0:64], m2b, ALU.mult)
        for p in PARTS:
            zc_ = zh(ZC, p)
            for jj in range(SP):
                P.mm(zc_[:, jj, 64:128], hv(ATk, p)[:, jj, 0:64], Vtm[:, SP * p + jj, :])
        for p in PARTS:
            P.copy("act", hv(X, p)[:, :, 0:64], Atm[:, SP * p:SP * p + SP, :])
            P.copy("dve", hv(X, p)[:, :, 64:128], zh(ZC, p)[:, :, 64:128])
        for it in range(6):
            Z = ZA if it % 2 == 0 else ZB
            for p in PARTS:
                z_, zc_, x_ = zh(Z, p), zh(ZC, p), hv(X, p)
                qq_prev = hv(QQ[(it - 1) % 2], p)
                for jj in range(SP):
                    qt = hv(ATb, p)[:, jj, 0:64] if it == 0 else qq_prev[:, jj, 64:128]
                    P.mm(z_[:, jj, :], qt, x_[:, jj, :])
                if it < 5:
                    for jj in range(SP):
                        q = hv(Nm, p)[:, jj, :] if it == 0 else qq_prev[:, jj, 0:64]
                        qt = hv(ATb, p)[:, jj, 0:64] if it == 0 else qq_prev[:, jj, 64:128]
                        if it < 4:
                            P.mm(zc_[:, jj, 0:64], qt, q)
                        P.mm(zc_[:, jj, 64:128], q, qt)
            for p in PARTS:
                if it < 4:
                    P.copy("act", hv(QQ[it % 2], p), zh(ZC, p))
                elif it == 4:
                    P.copy("act", hv(QQ[it % 2], p)[:, :, 64:128], zh(ZC, p)[:, :, 64:128])
                P.tt("dve", hv(X, p), hv(X, p), zh(Z, p), ALU.add)
        for p in PARTS:
            za_, x_ = zh(ZA, p), hv(X, p)
            for jj in range(SP):
                h = SP * p + jj
                P.mm(za_[:, jj, 0:64], x_[:, jj, 0:64], Bhtm[:, h, :])
                P.mm(za_[:, jj, 64:128], x_[:, jj, 0:64], hv(ATb, p)[:, jj, 64:128])
        for p in PARTS:
            P.tt("dve", hv(GE, p), zh(ZA, p), GEadd[:, c, SP * p:SP * p + SP, :], ALU.add)
        for p in PARTS:
            zb_, x_, ge_, ts_ = zh(ZB, p), hv(X, p), hv(GE, p), hv(Tst, p)
            for jj in range(SP):
                h = SP * p + jj
                P.mm(zb_[:, jj, 0:64], Bhtm[:, h, :], x_[:, jj, 64:128], start=True, stop=False)
                P.mm(zb_[:, jj, 0:64], Khtm[:, h, :], Vtm[:, h, :], start=False, stop=False)
                P.mm(zb_[:, jj, 0:64], ge_[:, jj, 0:64], ts_[:, jj, :], start=False, stop=True)
                P.mm(zb_[:, jj, 64:128], hv(ATb, p)[:, jj, 64:128], x_[:, jj, 64:128], start=True, stop=False)
                P.mm(zb_[:, jj, 64:128], hv(ATk, p)[:, jj, 64:128], Vtm[:, h, :], start=False, stop=False)
                P.mm(zb_[:, jj, 64:128], ge_[:, jj, 64:128], ts_[:, jj, :], start=False, stop=True)
        for p in PARTS:
            w_ = SP * 64
            e, j0 = (SP * p) // 4, (SP * p) % 4
            P.copy("dve", hv(Tst, p), zh(ZB, p)[:, :, 0:64])
            if not last_sweep:
                P.copy("dve", y0s[:, c, p * w_:(p + 1) * w_].re("p (j v) -> p j v", j=SP), zh(ZB, p)[:, :, 64:128])
            else:
                P.tt("dve", yt.re("p (j e v) -> p e j v", j=4, e=2)[:, e, j0:j0 + SP, :], zh(ZB, p)[:, :, 64:128],
                     y0s[:, c, p * w_:(p + 1) * w_].re("p (j v) -> p j v", j=SP), ALU.add)
        if last_sweep:
            y3 = lambda t_: t_.re("p (h v) -> p h v", h=8)
            P.reduce("dve", gst[:, 0:8], y3(yt))
            P.ts("dve", gst[:, 8:16], gst[:, 0:8], 1.0 / 64, None, op0=ALU.mult)
            P.tt("dve", y3(yc), y3(yt), gst[:, 8:16].us(2).bc([64, 8, 64]), ALU.subtract)
            P.act(ysq, yc, AF.Square)
            P.reduce("dve", gst[:, 16:24], y3(ysq))
            P.act(gst[:, 24:32], gst[:, 16:24], AF.Ln, scale=1.0 / 64, bias=kc[0:64, 2:3])
            P.act(gst[:, 32:40], gst[:, 24:32], AF.Exp, scale=-0.5)
            P.tt("dve", y3(yc), y3(yc), gst[:, 32:40].us(2).bc([64, 8, 64]), ALU.mult)
            P.tt("dve", yc, yc, lnw, ALU.mult)
            P.tt("dve", ynb, yc, lnb, ALU.add)
            for j in range(4):
                P.tr(k.psb[0][:, j * 64:(j + 1) * 64], ynb[:, j * 128:(j + 1) * 128], idb[0:64, 0:64])
            P.copy("act", ynT[:, :, c * 64:(c + 1) * 64], k.psb[0][:, 0:256].re("p (j t) -> p j t", j=4))

    blocks = []
    t_seq = 0
    for s, S in enumerate(seqs):
        nb = S // BLK
        for d in range(2):
            if d == 1 and k.rw_stop == "e_y0":
                continue
            order = range(nb) if d == 0 else range(nb - 1, -1, -1)
            for i_, b in enumerate(order):
                blocks.append(dict(s=s, d=d, b=b, nb=nb, t0=t_seq + b * BLK, first=(i_ == 0)))
        t_seq += S

    def load_conv(blk, conv_now):
        b, nb, t0 = blk["b"], blk["nb"], blk["t0"]
        z = zin[0]
        lo = 1 if b == 0 else 0
        hi = BLK + 1 if b == nb - 1 else BLK + 2
        if b == 0:
            P.memset("pool", z[:, :, 0:1], 0.0)
        if b == nb - 1:
            P.memset("pool", z[:, :, BLK + 1:BLK + 2], 0.0)
        P.dma(z[:, 0:15, lo:hi], TV(zrw[0:1920, t0 - 1 + lo:t0 - 1 + hi].rearrange("(c p) t -> p c t", p=128),
                                     "zrw_g%d" % (t0 // 512)), "xin0")
        P.dma(z[0:32, 15, lo:hi], TV(zrw[1920:1952, t0 - 1 + lo:t0 - 1 + hi], "zrw_g%d" % (t0 // 512)), "xin0")
        if conv_now:
            for c in range(16):
                conv_chunk(c)

    def conv_chunk(c):
        z = zin[0]
        np_ = 128 if c < 15 else 32
        o = zc[0:np_, c, :]
        P.act(o, z[0:np_, c, 0:BLK], AF.Copy, scale=cw[0:np_, c, 0:1])
        P.stt("dve", o, z[0:np_, c, 1:BLK + 1], cw[0:np_, c, 1:2], o, ALU.mult, ALU.add)
        P.stt("dve", o, z[0:np_, c, 2:BLK + 2], cw[0:np_, c, 2:3], o, ALU.mult, ALU.add)

    if blocks:
        load_conv(blocks[0], True)
    for bi, blk in enumerate(blocks):
        d, t0 = blk["d"], blk["t0"]
        if blk["first"]:
            for p_ in range(NP):
                P.memset("pool", hv(Tst, p_), 0.0)
        if d == 1:
            P.dma(y0s, TV(y0_d[t0:t0 + BLK, :].rearrange("(c p) v -> p c v", p=64), "y0_%d" % (t0 // 512)), "y0in")
        if k.rw_stop in ("load", "conv"):
            continue
        P.act(tw, zc[:, 12, :], AF.Tanh)
        P.copy("act", xab, zc[:, 13, :])
        if d == 1:
            P.act(sgg, zc[:, 14, :], AF.Sigmoid)
            P.act(sgg2, zc[0:32, 15, :], AF.Sigmoid)
        for j in range(4):
            prep_pair(j, d, None, [0, 1] if d == 1 else [0], d == 1)
        have_next = bi + 1 < len(blocks)
        if have_next:
            load_conv(blocks[bi + 1], False)
        if k.rw_stop == "prep":
            continue
        corder = range(NC) if d == 0 else range(NC - 1, -1, -1)
        for ci_, c in enumerate(corder):
            unit_chunk(c, d, t0 + c * 64, d == 1)
            if have_next:
                conv_chunk(2 * ci_)
                conv_chunk(2 * ci_ + 1)
        if d == 0:
            P.dma(TV(y0_d[t0:t0 + BLK, :].rearrange("(c p) v -> p c v", p=64), "y0_%d" % (t0 // 512)), y0s, "y0out")
        else:
            P.tt("dve", ynT, ynT, cb, ALU.add)
            P.tt("dve", yo, ynT, gT, ALU.mult)
            P.dma(TV(yrwT_d[:, t0:t0 + BLK].rearrange("(j p) t -> p j t", p=128), "yrwT_%d" % (t0 // 512)), yo, "yrwout")
    P.barrier()


def na_phase(P, k, seqs, qT, kT, vna, ynaT_d, w):
    A = k.arena
    A.reset()
    Bt = A.alloc("Bt", [128, 8, 16, 64], F32)
    idq = A.alloc("idq", [64, 64], F32)
    qg = [A.alloc("qg%d" % i, [128, 4, 512]) for i in range(2)]
    kw = [A.alloc("kw%d" % i, [128, 4, 1024]) for i in range(2)]
    vE = [A.alloc("vE%d" % i, [128, 8, 512]) for i in range(2)]
    vO = [A.alloc("vO%d" % i, [128, 8, 512]) for i in range(2)]
    sb = [A.alloc("sb%d" % i, [128, 1024], F32) for i in range(2)]
    PT = [A.alloc("PT%d" % i, [128, 4, 4, 64]) for i in range(2)]
    dtmp = A.alloc("dtmp", [64, 512], F32)
    den = A.alloc("den", [64, 16], F32)
    ynb = A.alloc("ynb", [64, 512])
    yst = [A.alloc("yst%d" % i, [128, 4, 512]) for i in range(2)]
    P.dma(Bt, TV(w["na_bias"], "na_bias"), "misc")
    P.dma(idq, TV(w["idst"][0:64, :], "idst"), "misc")
    ST = [k.psbig[0], k.psbig[1]]
    OUT = k.ps[4]
    DEN = k.ps[5]
    t_seq = 0
    gi = 0
    for s, S in enumerate(seqs):
        R = S // 64
        for g in range(R // 8):
            b = gi % 2
            rk0 = min(max(8 * g - 4, 0), R - 16)
            t0 = t_seq + g * 512
            tk = t_seq + rk0 * 64
            P.dma(qg[b], TV(qT[:, t0:t0 + 512].rearrange("(j p) t -> p j t", p=128), "qT_g%d" % (t0 // 512)), "xin%d" % b)
            for kk_ in range(2):
                P.dma(kw[b][:, :, kk_ * 512:(kk_ + 1) * 512],
                      TV(kT[:, tk + kk_ * 512:tk + (kk_ + 1) * 512].rearrange("(j p) t -> p j t", p=128),
                         "kT_g%d" % ((tk + kk_ * 512) // 512)), "xin%d" % b)
            P.dma(vE[b], TV(vna[tk:tk + 1024, :].rearrange("(i p) c -> p i c", p=128), "vna_w%d" % (tk // 512)), "vin%d" % b)
            P.dma(vO[b][:, 0:7, :], TV(vna[tk + 64:tk + 64 + 896, :].rearrange("(i p) c -> p i c", p=128),
                                       "vna_w%d" % (tk // 512)), "vin%d" % b)
            ys = yst[b]
            for lr in range(8):
                r = 8 * g + lr
                r0 = min(max(r - 4, 0), R - 8)
                o = r0 - rk0
                pt = PT
                for j in range(4):
                    koff = (o + 2 * j) * 64
                    for jh in range(4):
                        for e in range(2):
                            hs = slice(64 * e, 64 * e + 64)
                            P.mm(ST[e][:, (j * 4 + jh) * 64:(j * 4 + jh + 1) * 64], kw[b][hs, jh, koff:koff + 128],
                                 qg[b][hs, jh, lr * 64:(lr + 1) * 64])
                for e in range(2):
                    for j in range(4):
                        qd = r - (r0 + 2 * j) + 7
                        P.tt("dve", sb[e][:, j * 256:(j + 1) * 256].re("p (a q) -> p a q", a=4),
                             ST[e][:, j * 256:(j + 1) * 256].re("p (a q) -> p a q", a=4),
                             Bt.re("p (a e) d q -> p a e d q", e=2)[:, :, e, qd, :], ALU.add)
                    P.act(pt[e].re("p j a q -> p (j a q)"), sb[e], AF.Exp)
                for h in range(8):
                    jh, e = h // 2, h % 2
                    for j in range(4):
                        ci = o + 2 * j
                        vt = vE[b][:, ci // 2, :] if ci % 2 == 0 else vO[b][:, (ci - 1) // 2, :]
                        P.mm(OUT[0:64, h * 64:(h + 1) * 64], pt[e][:, j, jh, :], vt[:, h * 64:(h + 1) * 64],
                             start=(j == 0), stop=(j == 3))
                for e in range(2):
                    for j in range(4):
                        P.mm(DEN[0:64, e * 256:(e + 1) * 256], k.ones[:, 0:64], pt[e][:, j, :, :].re("p a q -> p (a q)"),
                             start=(j == 0), stop=(j == 3))
                P.tt("dve", dtmp.re("p (a q) -> p a q", a=8), DEN[0:64, :].re("p (a q) -> p a q", a=8),
                     idq.us(1).bc([64, 8, 64]), ALU.mult)
                P.reduce("dve", den[:, 0:8], dtmp.re("p (a q) -> p a q", a=8))
                P.recip(den[:, 8:16], den[:, 0:8])
                P.tt("dve", ynb.re("p (a e v) -> p e a v", a=4, e=2), OUT[0:64, :].re("p (a e v) -> p e a v", a=4, e=2),
                     den[:, 8:16].re("p (e a) -> p e a", e=2).us(3).bc([64, 2, 4, 64]), ALU.mult)
                for jh in range(4):
                    P.tr(k.psb[0][:, jh * 64:(jh + 1) * 64], ynb[:, jh * 128:(jh + 1) * 128], k.ident[0:64, 0:64])
                P.copy("act", ys[:, :, lr * 64:(lr + 1) * 64], k.psb[0][:, 0:256].re("p (j t) -> p j t", j=4))
            P.dma(TV(ynaT_d[:, t0:t0 + 512].rearrange("(j p) t -> p j t", p=128), "ynaT_%d" % (t0 // 512)), ys,
                  "ynaout%d" % b)
            gi += 1
        t_seq += S
    P.barrier()


def merge_phase(P, k, T, x1, ynaT, yrwT, ymT, gatesT, w):
    A = k.arena
    A.reset()
    G = 512
    WO = [A.alloc("WO%d" % i, [128, 4, D]) for i in range(3)]
    WOUT = A.alloc("WOUT", [128, 8, D])
    yb = [[A.alloc("yb%d_%d" % (i, b), [128, 4, G]) for b in range(3)] for i in range(2)]
    gt = [A.alloc("gt%d" % i, [128, 24, G]) for i in range(2)]
    xg = [A.alloc("xg%d" % i, [128, 4, D], F32) for i in range(2)]
    m0 = [A.alloc("m0_%d" % i, [128, G], F32) for i in range(2)]
    tm = [A.alloc("tm_%d" % i, [128, G], F32) for i in range(2)]
    mb = A.alloc("mb", [128, 8, G])
    for i, nm in enumerate(("w_o_na", "w_o_rw", "w_o_mem")):
        load_weight_cast(P, WO[i], w[nm], 4, nm, "wload")
    load_weight_cast(P, WOUT, w["w_out"], 8, "w_out", "wload")
    srcs = (ynaT, yrwT, ymT)
    ps = k.ps
    for g in range(T // G):
        b = g % 2
        t0 = g * G
        for i in range(3):
            P.dma(yb[b][i], TV(srcs[i][:, t0:t0 + G].rearrange("(j p) t -> p j t", p=128), "ysrc%d_%d" % (i, g)), "xin%d" % b)
        P.dma(gt[b], TV(gatesT[:, t0:t0 + G].rearrange("(c p) t -> p c t", p=128), "gates_g%d" % g), "gin%d" % b)
        P.dma(xg[b], TV(x1[t0:t0 + G, :].rearrange("(i p) d -> p i d", p=128), "x1_g%d" % g), "x1in%d" % b)
        for oc in range(8):
            for i in range(3):
                pz = ps[(oc * 3 + i) % 4]
                for kc in range(4):
                    P.mm(pz, WO[i][:, kc, oc * 128:(oc + 1) * 128], yb[b][i][:, kc, :], start=(kc == 0), stop=(kc == 3))
                if i == 0:
                    P.tt("dve", m0[oc % 2], pz, gt[b][:, oc, :], ALU.mult)
                else:
                    P.tt("dve", tm[i % 2], pz, gt[b][:, i * 8 + oc, :], ALU.mult)
                    if i == 1:
                        P.tt("dve", m0[oc % 2], m0[oc % 2], tm[1], ALU.add)
                    else:
                        P.tt("dve", mb[:, oc, :], m0[oc % 2], tm[0], ALU.add)
        for i in range(4):
            for n in range(2):
                pd = ps[4 + (2 * i + n) % 2]
                for kc in range(8):
                    P.mm(pd, mb[:, kc, i * 128:(i + 1) * 128], WOUT[:, kc, n * 512:(n + 1) * 512],
                         start=(kc == 0), stop=(kc == 7))
                P.tt("dve", xg[b][:, i, n * 512:(n + 1) * 512], pd, xg[b][:, i, n * 512:(n + 1) * 512], ALU.add)
        P.dma(TV(x1[t0:t0 + G, :].rearrange("(i p) d -> p i d", p=128), "x1_g%d" % g), xg[b], "x1out%d" % b)
    P.barrier()


def build(seqs, phases="ABCNDE", dbg=(), rw_stop=None):
    T = sum(seqs)
    nc = bass.Bass("TRN2", target_bir_lowering=False)
    k = K()
    k.nc = nc
    k.rw_stop = rw_stop

    def din(name, shape, dt=F32):
        return nc.dram_tensor(name, list(shape), dt, kind="ExternalInput").ap()

    def dscr(name, shape, dt):
        kind = "ExternalOutput" if name in dbg else "Internal"
        return nc.dram_tensor(name, list(shape), dt, kind=kind).ap()

    x = din("x", [T, D])
    y = nc.dram_tensor("y", [T, D], F32, kind="ExternalOutput").ap()
    ident_d = din("ident", [128, 128], BF16)
    w = {}
    for nm, shp in (("w_ffn1_gate", [D, DFF]), ("w_ffn1_up", [D, DFF]), ("w_ffn1_down", [DFF, D]),
                    ("w_ffn2_gate", [D, DFF]), ("w_ffn2_up", [D, DFF]), ("w_ffn2_down", [DFF, D]),
                    ("g_ffn1_c", [128, 8]), ("g_ffn2_c", [128, 8]), ("w_in", [D, NIN]), ("g_mix_c", [128, 8]),
                    ("gvec_q", [1, 1536]), ("mem", [256 * len(seqs), D]), ("w_mem_kv", [D, D]),
                    ("g_mem_c", [128, 8]), ("gvec_km", [1, 512]),
                    ("w2_rw", [128, 512]), ("a2_rw", [128, 512]), ("g2_rw", [160, 512]), ("conv_c", [128, 16, 3]),
                    ("rw_pc", [128, 32]), ("rmask", [128, 512]), ("idst", [128, 64]),
                    ("m1", [64, 2, 128]), ("m2", [64, 2, 64]), ("ln_w", [1, 512]), ("ln_b", [1, 512])):
        w[nm] = din(nm, shp)
    w["bones"] = din("bones", [128, 128], BF16)
    for nm, shp in (("na_bias", [128, 8, 16, 64]), ("w_o_na", [512, D]), ("w_o_rw", [512, D]), ("w_o_mem", [512, D]),
                    ("w_out", [D, D])):
        w[nm] = din(nm, shp)
    ynaT = dscr("ynaT", [512, T], BF16)
    y0_d = dscr("y0", [T, 512], F32)
    yrwT = dscr("yrwT", [512, T], BF16)
    x1 = dscr("x1", [T, D], F32)
    h2T = dscr("h2T", [D, T], BF16)
    qT = dscr("qT", [512, T], BF16)
    kT = dscr("kT", [512, T], BF16)
    vna = dscr("vna", [T, 512], BF16)
    zrw = dscr("zrw", [NRW, T], BF16)
    ymT = dscr("ymT", [512, T], BF16)
    gatesT = dscr("gatesT", [3072, T], BF16)

    with ExitStack() as st:
        arena_t = st.enter_context(nc.sbuf_tensor("arena", [128, 98000], BF16))
        k.arena = Arena(arena_t[:, :], 196000)
        cst = st.enter_context(nc.sbuf_tensor("cst", [128, 8192], BF16))
        k.ident = TV(cst[:, 0:128], "ident")
        barscr = TV(cst[:, 128:192], "barscr")
        k.ones = TV(cst[:, 256:384], "ones")
        k.kmT = [TV(cst[:, 512 + i * 1024:512 + (i + 1) * 1024].rearrange("p (h m) -> p h m", h=4), "kmT%d" % i)
                 for i in range(2)]
        k.vm = [TV(cst[:, 2560 + i * 1024:2560 + (i + 1) * 1024].rearrange("p (c v) -> p c v", c=2), "vm%d" % i)
                for i in range(2)]
        big = [st.enter_context(nc.psum_tensor("psbig%d" % i, [128, 1024], F32)) for i in range(3)]
        k.psbig = [TV(big[i][:, :], "psbig%d" % i) for i in range(3)]
        k.ps = [TV(big[i // 2][:, (i % 2) * 512:(i % 2 + 1) * 512], "ps%d" % i) for i in range(6)]
        k.psb = [TV(st.enter_context(nc.psum_tensor("psb%d" % i, [128, 1024], BF16))[:, :], "psb%d" % i)
                 for i in range(2)]
        P = Prog(nc)
        bs = barscr.ap
        psbar = k.ps[5].ap
        idap = k.ident.ap
        P.bar_fns = {
            "pe": lambda e: e.matmul(psbar[0:64, 0:32], lhsT=idap[0:64, 0:64], rhs=idap[0:64, 0:32], start=True, stop=True),
            "act": lambda e: e.activation(out=bs[:, 0:8], in_=bs[:, 8:16], func=AF.Copy),
            "dve": lambda e: e.tensor_copy(out=bs[:, 16:24], in_=bs[:, 24:32]),
            "pool": lambda e: e.tensor_copy(out=bs[:, 32:40], in_=bs[:, 40:48]),
        }
        P.add("pool", lambda e: e.memset(bs, 0.0), writes=["bs_act", "bs_dve", "bs_pool"])
        P.memset("pool", k.ones, 1.0)
        P.dma(k.ident, TV(ident_d, "ident_d"), "misc")
        P.barrier()
        if "A" in phases:
            ffn_phase(P, k, "A", x, x1, T, w["w_ffn1_gate"], w["w_ffn1_up"], w["w_ffn1_down"], w["g_ffn1_c"],
                      h2T_dram=h2T)
        if "B" in phases:
            memkv_phase(P, k, seqs, w["mem"], w["w_mem_kv"], w["g_mem_c"], w["gvec_km"])
            proj_phase(P, k, seqs, h2T, w["w_in"], w["g_mix_c"], w["gvec_q"], qT, kT, vna, zrw, ymT, gatesT)
        if "C" in phases:
            rwkv_phase(P, k, seqs, zrw, y0_d, yrwT, w)
        if "N" in phases:
            na_phase(P, k, seqs, qT, kT, vna, ynaT, w)
        if "D" in phases:
            merge_phase(P, k, T, x1, ynaT, yrwT, ymT, gatesT, w)
        if "E" in phases:
            ffn_phase(P, k, "E", x1, y, T, w["w_ffn2_gate"], w["w_ffn2_up"], w["w_ffn2_down"], w["g_ffn2_c"])
        P.finalize(st)
        k.P = P
    return nc, k


def _gcol(g):
    return np.ascontiguousarray(np.asarray(g, np.float32).reshape(8, 128).T)


def make_inputs(d, b, seqdefs):
    f = lambda n: np.ascontiguousarray(np.asarray(d[n][0], np.float32))
    inp = {"x": np.ascontiguousarray(np.concatenate([np.asarray(d[xn][b, :S]) for xn, mn, S in seqdefs], 0)),
           "mem": np.ascontiguousarray(np.concatenate([np.asarray(d[mn][b]) for xn, mn, S in seqdefs], 0)),
           "ident": np.eye(128).astype(ml_dtypes.bfloat16)}
    for n in ("w_ffn1_gate", "w_ffn1_up", "w_ffn1_down", "w_ffn2_gate", "w_ffn2_up", "w_ffn2_down", "w_in",
              "w_mem_kv"):
        inp[n] = f(n)
    inp["g_ffn1_c"] = _gcol(d["g_ffn1"][0])
    inp["g_ffn2_c"] = _gcol(d["g_ffn2"][0])
    inp["g_mix_c"] = _gcol(d["g_mix"][0])
    inp["g_mem_c"] = _gcol(d["g_mem_norm"][0])
    inp["gvec_q"] = np.concatenate([np.tile(f("g_qn_na"), 8), np.tile(f("g_kn_na"), 8),
                                    np.tile(f("g_qn_mem"), 4)])[None, :].astype(np.float32)
    inp["gvec_km"] = np.tile(f("g_kn_mem"), 4)[None, :].astype(np.float32)
    inp["w2_rw"] = f("w2_rw").reshape(128, 512)
    inp["a2_rw"] = f("a2_rw").reshape(128, 512)
    inp["g2_rw"] = f("g2_rw")
    cv = np.zeros((3, 2048), np.float32)
    cv[:, :1952] = f("conv_rw")
    inp["conv_c"] = np.ascontiguousarray(cv.reshape(3, 16, 128).transpose(2, 1, 0))
    col = lambda v: np.asarray(v, np.float32).reshape(-1, 128).T
    inp["rw_pc"] = np.ascontiguousarray(np.concatenate(
        [col(f("w0_rw").reshape(-1)), col(f("a0_rw").reshape(-1)), col(f("k_k_rw")), col(f("k_a_rw")),
         col(f("r_k_rw").reshape(-1))], axis=1))
    rm = np.ones((128, 512), np.float32)
    rm[:, ::64] = 0.0
    inp["rmask"] = rm
    inp["idst"] = np.concatenate([np.eye(64), np.eye(64)], 0).astype(np.float32)
    bo = np.zeros((128, 128), np.float32)
    bo[:64, :64] = 1.0
    bo[64:, 64:] = 1.0
    inp["bones"] = bo.astype(ml_dtypes.bfloat16)
    idx = np.arange(64)
    m1 = np.zeros((64, 2, 128), np.float32)
    m1[:, 0, :64] = idx[:, None] < idx[None, :]
    m1[:, 0, 64:] = idx[:, None] <= idx[None, :]
    m1[:, 1, :64] = idx[:, None] > idx[None, :]
    m1[:, 1, 64:] = idx[:, None] >= idx[None, :]
    m2 = np.zeros((64, 2, 64), np.float32)
    m2[:, 0, :] = idx[None, :] < idx[:, None]
    m2[:, 1, :] = idx[None, :] > idx[:, None]
    inp["m1"] = m1
    inp["m2"] = m2
    for n in ("w_o_na", "w_o_rw", "w_o_mem", "w_out"):
        inp[n] = f(n)
    rpb = f("rpb_na")
    cols = np.arange(64)
    cs = np.clip(cols - 8, 0, 48)
    kc_, qc_ = np.meshgrid(cols, cols, indexing="ij")
    valid = (kc_ >= cs[None, :]) & (kc_ < cs[None, :] + 16)
    dc = np.clip(kc_ - qc_ + 15, 0, 30)
    tab = np.full((2, 64, 8, 16, 64), -30000.0, np.float32)
    for jj in range(2):
        for qd in range(16):
            dr = jj + 7 - qd
            if -7 <= dr <= 7:
                vals = rpb[:, dr + 7, :][:, dc]
                tab[jj, :, :, qd, :] = np.where(valid[:, None, :], vals.transpose(1, 0, 2), -30000.0)
    inp["na_bias"] = np.ascontiguousarray(tab.reshape(128, 8, 16, 64))
    inp["ln_w"] = f("ln_x_w_rw")[None, :]
    inp["ln_b"] = f("ln_x_b_rw")[None, :]
    return inp


SEQS = (4096, 8192)


def kernel(**inp):
    n = 8
    nc, _ = build(list(SEQS))
    in_maps = [make_inputs(inp, b, [("x_prompt", "mem_prompt", SEQS[0]), ("x_sample", "mem_sample", SEQS[1])])
               for b in range(n)]
    res = run_bass_kernel_spmd(nc, in_maps, core_ids=list(range(n)))
    ys = [np.asarray(r["y"], np.float32) for r in res.results]
    y_prompt = np.stack([y[:SEQS[0]] for y in ys], 0)
    y_sample = np.stack([y[SEQS[0]:] for y in ys], 0)
    return (y_prompt, y_sample)
```

```python
from contextlib import ExitStack
import numpy as np
import ml_dtypes
import concourse.bass as bass
import concourse.mybir as mybir
from concourse.bass_utils import run_bass_kernel_spmd

F32 = mybir.dt.float32
BF16 = mybir.dt.bfloat16
AF = mybir.ActivationFunctionType
ALU = mybir.AluOpType
AX = mybir.AxisListType
SEM_EPOCH = 20000
D = 1024
DFF = 2816
NFF = DFF // 128
NIN = 7072
EPS = 1e-6


class Buf:
    __slots__ = ("name", "last_w", "readers")

    def __init__(self, name):
        self.name = name
        self.last_w = None
        self.readers = {}


class Op:
    __slots__ = ("eng", "fn", "deps", "is_dma", "need_inc", "sem", "val", "dsem", "idx")

    def __init__(self, eng, fn, is_dma=False, dsem=None):
        self.eng = eng
        self.fn = fn
        self.deps = []
        self.is_dma = is_dma
        self.need_inc = False
        self.sem = None
        self.val = None
        self.dsem = dsem
        self.idx = None


class TV:
    __slots__ = ("ap", "name")

    def __init__(self, ap, name):
        self.ap = ap
        self.name = name

    def __getitem__(self, k):
        return TV(self.ap[k], self.name)

    def re(self, s, **kw):
        return TV(self.ap.rearrange(s, **kw), self.name)

    def bc(self, shape):
        return TV(self.ap.broadcast_to(shape), self.name)

    def us(self, ax):
        return TV(self.ap.unsqueeze(ax), self.name)

    def cast(self, dt):
        return TV(self.ap.bitcast(dt), self.name)


def _ap(x):
    return x.ap if isinstance(x, TV) else x


def _names(*xs):
    return [x.name for x in xs if isinstance(x, TV)]


class Prog:
    ENGS = ("pe", "act", "dve", "pool", "sp")

    def __init__(self, nc):
        self.nc = nc
        self.ops = {e: [] for e in self.ENGS}
        self.bufs = {}
        self.n_ops = 0
        self.last_dma = {}
        self.bar_fns = {}
        self.alias = {}

    def buf(self, name):
        b = self.bufs.get(name)
        if b is None:
            b = Buf(name)
            self.bufs[name] = b
        return b

    def add(self, eng, fn, reads=(), writes=(), dma=False, dsem=None):
        op = Op(eng, fn, is_dma=dma, dsem=dsem)
        op.idx = self.n_ops
        self.n_ops += 1
        deps = []
        if self.alias:
            reads = list(reads) + [x for b in reads for x in self.alias.get(b, ())]
            writes = list(writes) + [x for b in writes for x in self.alias.get(b, ())]
        reads = [self.buf(b) for b in reads]
        writes = [self.buf(b) for b in writes]
        for b in reads:
            if b.last_w is not None:
                deps.append((b.last_w, "raw"))
        for b in writes:
            if b.last_w is not None:
                deps.append((b.last_w, "waw"))
            for r in b.readers.values():
                deps.append((r, "war"))
        seen = set()
        for d, kind in deps:
            if d.is_dma:
                d = self.last_dma[d.dsem]
            if d is op or id(d) in seen:
                continue
            if (not d.is_dma) and (not dma) and d.eng == eng:
                if kind != "raw" or eng == "pe":
                    continue
            seen.add(id(d))
            op.deps.append(d)
            d.need_inc = True
        for b in reads:
            b.readers[("dma", op.idx) if dma else eng] = op
        for b in writes:
            b.last_w = op
            b.readers = {}
        self.ops[eng].append(op)
        if dma:
            self.last_dma[dsem] = op
        return op

    def mm(self, out, lhsT, rhs, start=True, stop=True):
        return self.add("pe", lambda e: e.matmul(out.ap, lhsT=lhsT.ap, rhs=rhs.ap, start=start, stop=stop),
                        reads=_names(lhsT, rhs), writes=[out.name])

    def tr(self, out, in_, ident):
        return self.add("pe", lambda e: e.transpose(out=out.ap, in_=in_.ap, identity=ident.ap),
                        reads=_names(in_, ident), writes=[out.name])

    def act(self, out, in_, func, scale=1.0, bias=None, accum=None, eng="act"):
        kw = {}
        if bias is not None:
            kw["bias"] = _ap(bias)
        if accum is not None:
            kw["accum_out"] = accum.ap
        w = [out.name] + ([accum.name] if accum is not None else [])
        return self.add(eng, lambda e: e.activation(out=out.ap, in_=in_.ap, func=func, scale=_ap(scale), **kw),
                        reads=_names(in_, scale, bias), writes=w)

    def tt(self, eng, out, a, b, op):
        return self.add(eng, lambda e: e.tensor_tensor(out=out.ap, in0=a.ap, in1=b.ap, op=op),
                        reads=_names(a, b), writes=[out.name])

    def ts(self, eng, out, a, s1, s2=None, op0=ALU.mult, op1=None):
        kw = {} if op1 is None else {"op1": op1}
        return self.add(eng, lambda e: e.tensor_scalar(out=out.ap, in0=a.ap, scalar1=_ap(s1), scalar2=_ap(s2),
                                                       op0=op0, **kw),
                        reads=_names(a, s1, s2), writes=[out.name])

    def stt(self, eng, out, a, s, b, op0, op1):
        return self.add(eng, lambda e: e.scalar_tensor_tensor(out=out.ap, in0=a.ap, scalar=_ap(s), in1=b.ap,
                                                              op0=op0, op1=op1),
                        reads=_names(a, s, b), writes=[out.name])

    def copy(self, eng, out, in_):
        if eng == "act":
            return self.act(out, in_, AF.Copy)
        return self.add(eng, lambda e: e.tensor_copy(out=out.ap, in_=in_.ap), reads=[in_.name], writes=[out.name])

    def memset(self, eng, out, val):
        return self.add(eng, lambda e: e.memset(out.ap, val), writes=[out.name])

    def reduce(self, eng, out, in_, op=ALU.add, axis=AX.X):
        return self.add(eng, lambda e: e.tensor_reduce(out=out.ap, in_=in_.ap, axis=axis, op=op),
                        reads=[in_.name], writes=[out.name])

    def recip(self, out, in_):
        return self.add("dve", lambda e: e.reciprocal(out=out.ap, in_=in_.ap), reads=[in_.name], writes=[out.name])

    def scan(self, out, d0, d1, init, op0, op1):
        return self.add("dve", lambda e: e.tensor_tensor_scan(out=out.ap, data0=d0.ap, data1=d1.ap, initial=_ap(init),
                                                              op0=op0, op1=op1),
                        reads=_names(d0, d1, init), writes=[out.name])

    def dma(self, out, in_, dsem, eng="sp"):
        return self.add(eng, lambda e: e.dma_start(out=out.ap, in_=in_.ap), reads=[in_.name], writes=[out.name],
                        dma=True, dsem=dsem)

    def barrier(self):
        r1 = []
        for e in ("pe", "act", "dve", "pool"):
            op = self.add(e, self.bar_fns[e], reads=["ident" if e == "pe" else "bs_" + e],
                          writes=["bs_" + e] + (["ps5", "psbig2"] if e == "pe" else []))
            op.need_inc = True
            r1.append(op)
        extra = r1 + list(self.last_dma.values())
        for e in self.ENGS:
            fn = self.bar_fns[e] if e != "sp" else (lambda eng: eng.nop())
            op = self.add(e, fn, reads=[] if e == "sp" else ["ident" if e == "pe" else "bs_" + e],
                          writes=["bs_" + e] + (["ps5", "psbig2"] if e == "pe" else []))
            have = set(id(d) for d in op.deps)
            for d in extra:
                if id(d) not in have and not (d.eng == e and not d.is_dma):
                    op.deps.append(d)
                    d.need_inc = True
        keep = {n: self.bufs[n] for n in ("ident", "bs_act", "bs_dve", "bs_pool", "bs_pe", "bs_sp") if n in self.bufs}
        self.bufs = keep

    def finalize(self, stack):
        nc = self.nc
        sem_cache = {}

        def get_sem(key):
            s = sem_cache.get(key)
            if s is None:
                s = stack.enter_context(nc.semaphore("s%d" % len(sem_cache)))
                sem_cache[key] = s
            return s

        dma_counts = {}
        final_tokens = {}
        all_ops = sorted((op for e in self.ENGS for op in self.ops[e]), key=lambda o: o.idx)
        for op in all_ops:
            if op.is_dma:
                key = ("d", op.dsem)
                c = dma_counts.get(key, 0) + 16
                dma_counts[key] = c
                op.sem = key
                op.val = c
                final_tokens[key] = c
        for e in self.ENGS:
            cnt = 0
            epoch = 0
            for op in self.ops[e]:
                if (not op.is_dma) and op.need_inc:
                    cnt += 1
                    if cnt > SEM_EPOCH:
                        epoch += 1
                        cnt = 1
                    op.sem = ("e", e, epoch)
                    op.val = cnt
        for op in all_ops:
            if op.sem is not None:
                get_sem(op.sem)
        block = stack.enter_context(nc.Block())
        engmap = {"pe": block.tensor, "act": block.scalar, "dve": block.vector,
                  "pool": block.gpsimd, "sp": block.sync}
        self.n_waits = 0

        def make_stream(e):
            ops = self.ops[e]

            def stream(eng):
                waited = {}
                for op in ops:
                    need = {}
                    for d in op.deps:
                        k = d.sem
                        if k[0] == "e":
                            cur = waited.get(k[1], (-1, 0))
                            if (k[2], d.val) <= cur:
                                continue
                        elif waited.get(k, 0) >= d.val:
                            continue
                        prev = need.get(k)
                        if prev is None or d.val > prev:
                            need[k] = d.val
                    for k, v in sorted(need.items(), key=lambda kv: str(kv[0])):
                        eng.wait_ge(sem_cache[k], v)
                        self.n_waits += 1
                        if k[0] == "e":
                            cur = waited.get(k[1], (-1, 0))
                            if (k[2], v) > cur:
                                waited[k[1]] = (k[2], v)
                        else:
                            waited[k] = v
                    ins = op.fn(eng)
                    if op.is_dma:
                        ins.then_inc(sem_cache[op.sem], 16)
                    elif op.need_inc and op.sem is not None:
                        ins.then_inc(sem_cache[op.sem], 1)
                if e == "sp":
                    for k, v in final_tokens.items():
                        eng.wait_ge(sem_cache[k], v)
            return stream

        for e in self.ENGS:
            engmap[e](make_stream(e))


class Arena:
    def __init__(self, base_ap, nbytes):
        self.base = base_ap
        self.nbytes = nbytes
        self.off = 0
        self.gen = 0

    def reset(self):
        self.off = 0
        self.gen += 1

    def alloc(self, name, shape, dt=BF16):
        esz = 4 if dt == F32 else 2
        n = 1
        for s in shape[1:]:
            n *= s
        size = (n * esz + 63) // 64 * 64
        a = self.off
        self.off += size
        assert self.off <= self.nbytes, "arena overflow %s %d" % (name, self.off)
        v = self.base[0:shape[0], a // 2:(a + n * esz) // 2]
        if dt != BF16:
            v = v.bitcast(dt)
        if len(shape) == 3:
            v = v.rearrange("p (a b) -> p a b", a=shape[1])
        elif len(shape) == 4:
            v = v.rearrange("p (a b c) -> p a b c", a=shape[1], b=shape[2])
        return TV(v, "%s#%d" % (name, self.gen))


class K:
    pass


def rms_transpose(P, k, x_tv, out_hT, junk, stat, xn, ptr, nfeat=D):
    nchunk = nfeat // 128
    P.memset("pool", stat[:, 0:1], 0.0)
    P.act(junk, x_tv, AF.Square, accum=stat[:, 0:1])
    P.ts("dve", stat[:, 1:2], stat[:, 0:1], 1.0 / nfeat, EPS, op0=ALU.mult, op1=ALU.add)
    P.act(stat[:, 2:3], stat[:, 1:2], AF.Ln)
    P.act(stat[:, 3:4], stat[:, 2:3], AF.Exp, scale=-0.5)
    P.ts("dve", xn, x_tv, stat[:, 3:4], None, op0=ALU.mult)
    for c in range(nchunk):
        P.tr(ptr[:, c * 128:(c + 1) * 128], xn[:, c * 128:(c + 1) * 128], k.ident)
    P.copy("act", out_hT, ptr[:, 0:nfeat].re("p (c t) -> p c t", c=nchunk))


def load_weight_cast(P, dst, src_ap, nrow_chunks, name, dsem):
    for c in range(nrow_chunks):
        P.dma(dst[:, c, :], TV(src_ap[c * 128:(c + 1) * 128, :], name), dsem, eng="pool")


def ffn_phase(P, k, tag, src, dst, T, wg, wu, wd, gcol, h2T_dram=None):
    A = k.arena
    A.reset()
    G = 256
    ng = T // G
    WG = A.alloc("WG", [128, 8, DFF])
    WU = A.alloc("WU", [128, 8, DFF])
    WD = A.alloc("WD", [128, NFF, D])
    gc = A.alloc("gc", [128, 8], F32)
    xs = [A.alloc("x%d" % i, [128, 2, D], F32) for i in range(2)]
    hT = A.alloc("hT", [128, 8, G])
    actT = A.alloc("actT", [128, NFF, G])
    sg = [A.alloc("sg%d" % i, [128, G], F32) for i in range(2)]
    junk = A.alloc("junk", [128, D], F32)
    xn = A.alloc("xn", [128, D])
    stat = A.alloc("stat", [128, 8], F32)
    h2s = [A.alloc("h2s%d" % i, [128, 8, G]) for i in range(2)] if h2T_dram is not None else None
    P.dma(gc, TV(gcol, "gcol"), "misc")
    load_weight_cast(P, WG, wg, 8, "wg", "wload")
    load_weight_cast(P, WU, wu, 8, "wu", "wload")
    load_weight_cast(P, WD, wd, NFF, "wd", "wload")
    for c in range(8):
        P.ts("dve", WG[:, c, :], WG[:, c, :], gc[:, c:c + 1], None, op0=ALU.mult)
        P.ts("dve", WU[:, c, :], WU[:, c, :], gc[:, c:c + 1], None, op0=ALU.mult)
    psb = k.psb
    ps = k.ps
    def load_x(g_):
        P.dma(xs[g_ % 2], TV(src[g_ * G:(g_ + 1) * G, :].rearrange("(i p) d -> p i d", p=128), "%s_src%d" % (tag, g_)),
              "xin%d" % (g_ % 2))

    load_x(0)
    for g in range(ng):
        xg = xs[g % 2]
        t0 = g * G
        if g + 1 < ng:
            load_x(g + 1)
        for i in range(2):
            rms_transpose(P, k, xg[:, i, :], hT[:, :, i * 128:(i + 1) * 128], junk, stat, xn, psb[i % 2])
        for f in range(NFF):
            pg = ps[(2 * f) % 4]
            pu = ps[(2 * f + 1) % 4]
            for c in range(8):
                P.mm(pg[:, 0:G], WG[:, c, f * 128:(f + 1) * 128], hT[:, c, :], start=(c == 0), stop=(c == 7))
            for c in range(8):
                P.mm(pu[:, 0:G], WU[:, c, f * 128:(f + 1) * 128], hT[:, c, :], start=(c == 0), stop=(c == 7))
            s = sg[f % 2]
            P.act(s, pg[:, 0:G], AF.Silu)
            P.tt("dve", actT[:, f, :], s, pu[:, 0:G], ALU.mult)
        for i in range(2):
            for n in range(2):
                pd = ps[4 + (2 * i + n) % 2]
                for f in range(NFF):
                    P.mm(pd, actT[:, f, i * 128:(i + 1) * 128], WD[:, f, n * 512:(n + 1) * 512],
                         start=(f == 0), stop=(f == NFF - 1))
                P.stt("dve", xg[:, i, n * 512:(n + 1) * 512], pd, 0.5, xg[:, i, n * 512:(n + 1) * 512],
                      ALU.mult, ALU.add)
        P.dma(TV(dst[t0:t0 + G, :].rearrange("(i p) d -> p i d", p=128), "%s_dst%d" % (tag, g)), xg, "xout%d" % (g % 2))
        if h2T_dram is not None:
            h2 = h2s[g % 2]
            for i in range(2):
                rms_transpose(P, k, xg[:, i, :], h2[:, :, i * 128:(i + 1) * 128], junk, stat, xn, psb[i % 2])
            P.dma(TV(h2T_dram[:, t0:t0 + G].rearrange("(c p) t -> p c t", p=128), "h2T%d" % g), h2, "h2out%d" % (g % 2))
    P.barrier()


C_Q, C_K, C_V, C_RW, C_MQ, C_GATE = 0, 512, 1024, 1536, 3488, 4000
NRW = 1952


def head_norm(P, k, zsrc, nheads, hd, gvec, out_bf, sq, stat, scale_fix):
    n = nheads * hd
    P.act(sq[:, 0:n], zsrc, AF.Square)
    P.reduce("dve", stat[:, 0:nheads], sq[:, 0:n].re("p (h d) -> p h d", h=nheads))
    P.ts("dve", stat[:, 32:32 + nheads], stat[:, 0:nheads], 1.0 / hd, EPS, op0=ALU.mult, op1=ALU.add)
    P.act(stat[:, 64:64 + nheads], stat[:, 32:32 + nheads], AF.Ln)
    P.act(stat[:, 96:96 + nheads], stat[:, 64:64 + nheads], AF.Exp, scale=-0.5)
    for (h0, h1, f) in scale_fix:
        P.ts("dve", stat[:, 96 + h0:96 + h1], stat[:, 96 + h0:96 + h1], f, None, op0=ALU.mult)
    P.tt("dve", sq[:, 0:n].re("p (h d) -> p h d", h=nheads), zsrc.re("p (h d) -> p h d", h=nheads),
         stat[:, 96:96 + nheads].us(2).bc([128, nheads, hd]), ALU.mult)
    P.tt("dve", out_bf, sq[:, 0:n], gvec, ALU.mult)


def memkv_phase(P, k, seqs, mem, wkv, gmem_c, gvec_km):
    A = k.arena
    A.reset()
    WKV = A.alloc("WKV", [128, 8, D])
    gc = A.alloc("gc", [128, 8], F32)
    gkm = A.alloc("gkm", [128, 512], F32)
    mt = A.alloc("mt", [128, D], F32)
    junk = A.alloc("junk", [128, D], F32)
    xn = A.alloc("xn", [128, D])
    stat = A.alloc("stat", [128, 128], F32)
    mT = A.alloc("mT", [128, 8, 128])
    zk = A.alloc("zk", [128, 512], F32)
    sq = A.alloc("sq", [128, 512], F32)
    kn = A.alloc("kn", [128, 512])
    P.dma(gc, TV(gmem_c, "gmem_c"), "misc")
    P.dma(gkm, TV(gvec_km.partition_broadcast(128), "gvec_km"), "misc")
    load_weight_cast(P, WKV, wkv, 8, "wkv", "wload")
    for c in range(8):
        P.ts("dve", WKV[:, c, :], WKV[:, c, :], gc[:, c:c + 1], None, op0=ALU.mult)
    for s in range(len(seqs)):
        for i in range(2):
            P.dma(mt, TV(mem[s * 256 + i * 128:s * 256 + (i + 1) * 128, :], "mem"), "xin0")
            rms_transpose(P, k, mt, mT, junk, stat, xn, k.psb[0])
            for n in range(2):
                pz = k.ps[n]
                for c in range(8):
                    P.mm(pz, mT[:, c, :], WKV[:, c, n * 512:(n + 1) * 512], start=(c == 0), stop=(c == 7))
            P.copy("act", zk, k.ps[0])
            P.copy("dve", k.vm[s][:, i, :], k.ps[1])
            head_norm(P, k, zk, 4, 128, gkm, kn, sq, stat, [])
            for h in range(4):
                P.tr(k.psb[1][:, h * 128:(h + 1) * 128], kn[:, h * 128:(h + 1) * 128], k.ident)
            P.copy("act", k.kmT[s][:, :, i * 128:(i + 1) * 128], k.psb[1][:, 0:512].re("p (h t) -> p h t", h=4))
    P.barrier()


def proj_phase(P, k, seqs, h2T, w_in, gmix_c, gvec_q, qT, kT, vna, zrw, ymT_d, gatesT):
    A = k.arena
    A.reset()
    G = 512
    WIN = A.alloc("WIN", [128, 8, NIN])
    gc = A.alloc("gc", [128, 8], F32)
    gq = A.alloc("gq", [128, 1536], F32)
    hin = [A.alloc("hin%d" % i, [128, 8, G]) for i in range(2)]
    zq = A.alloc("zq", [128, 1536], F32)
    sq = A.alloc("sq", [128, 1536], F32)
    qn = A.alloc("qn", [128, 1536])
    stat = A.alloc("stat", [128, 128], F32)
    qkT = A.alloc("qkT", [128, 8, G])
    mqT = A.alloc("mqT", [128, 4, G])
    vst = A.alloc("vst", [128, 4, 512])
    PT = [A.alloc("PT%d" % i, [128, 2, G]) for i in range(2)]
    rc = A.alloc("rc", [128, G], F32)
    ymT = A.alloc("ymT", [128, 4, G])
    stg = [A.alloc("stg%d" % i, [128, 4, G]) for i in range(2)]
    P.dma(gc, TV(gmix_c, "gmix_c"), "misc")
    P.dma(gq, TV(gvec_q.partition_broadcast(128), "gvec_q"), "misc")
    load_weight_cast(P, WIN, w_in, 8, "w_in", "wload")
    for c in range(8):
        P.ts("dve", WIN[:, c, :], WIN[:, c, :], gc[:, c:c + 1], None, op0=ALU.mult)
    ps = k.ps
    gi = 0
    t0 = 0
    for s, S in enumerate(seqs):
        for gg in range(S // G):
            h = hin[gi % 2]
            P.dma(h, TV(h2T[:, t0:t0 + G].rearrange("(c p) t -> p c t", p=128), "h2T_g%d" % gi), "xin%d" % (gi % 2))
            for i in range(4):
                lt = lambda c: h[:, c, i * 128:(i + 1) * 128]
                for n, c0 in enumerate((C_Q, C_K, C_MQ, C_V)):
                    for c in range(8):
                        P.mm(ps[n], lt(c), WIN[:, c, c0:c0 + 512], start=(c == 0), stop=(c == 7))
                P.copy("act", zq[:, 0:512], ps[0])
                P.copy("act", zq[:, 512:1024], ps[1])
                P.copy("act", zq[:, 1024:1536], ps[2])
                P.copy("dve", vst[:, i, :], ps[3])
                head_norm(P, k, zq[:, 0:1024], 16, 64, gq[:, 0:1024], qn[:, 0:1024], sq, stat, [(0, 8, 0.125)])
                head_norm(P, k, zq[:, 1024:1536], 4, 128, gq[:, 1024:1536], qn[:, 1024:1536], sq, stat,
                          [(0, 4, 128.0 ** -0.5)])
                for c in range(8):
                    P.tr(k.psb[0][:, c * 128:(c + 1) * 128], qn[:, c * 128:(c + 1) * 128], k.ident)
                for c in range(4):
                    P.tr(k.psb[1][:, c * 128:(c + 1) * 128], qn[:, 1024 + c * 128:1024 + (c + 1) * 128], k.ident)
                P.copy("act", qkT[:, :, i * 128:(i + 1) * 128], k.psb[0][:, 0:1024].re("p (c t) -> p c t", c=8))
                P.copy("dve", mqT[:, :, i * 128:(i + 1) * 128], k.psb[1][:, 0:512].re("p (c t) -> p c t", c=4))
            P.dma(TV(qT[:, t0:t0 + G].rearrange("(c p) t -> p c t", p=128), "qT_g%d" % gi), qkT[:, 0:4, :], "qkout")
            P.dma(TV(kT[:, t0:t0 + G].rearrange("(c p) t -> p c t", p=128), "kT_g%d" % gi), qkT[:, 4:8, :], "qkout")
            P.dma(TV(vna[t0:t0 + G, :].rearrange("(i p) c -> p i c", p=128), "vna_g%d" % gi), vst, "vout")
            for hh in range(4):
                pt = PT[hh % 2]
                for mc in range(2):
                    P.mm(ps[4 + mc], k.kmT[s][:, hh, mc * 128:(mc + 1) * 128], mqT[:, hh, :])
                    P.act(pt[:, mc, :], ps[4 + mc], AF.Exp)
                for mc in range(2):
                    P.mm(ps[0], k.vm[s][:, mc, hh * 128:(hh + 1) * 128], pt[:, mc, :], start=(mc == 0), stop=(mc == 1))
                for mc in range(2):
                    P.mm(ps[1], k.ones, pt[:, mc, :], start=(mc == 0), stop=(mc == 1))
                P.recip(rc, ps[1])
                P.tt("dve", ymT[:, hh, :], ps[0], rc, ALU.mult)
            P.dma(TV(ymT_d[:, t0:t0 + G].rearrange("(c p) t -> p c t", p=128), "ymT_g%d" % gi), ymT, "ymout")
            chunks = [(C_RW + j * 128, min(128, NRW - j * 128), "rw", j) for j in range(16)]
            chunks += [(C_GATE + j * 128, 128, "gate", j) for j in range(24)]
            for ci, (c0, wd_, kind, j) in enumerate(chunks):
                pz = ps[2 + ci % 4]
                for c in range(8):
                    P.mm(pz[0:wd_, :], WIN[:, c, c0:c0 + wd_], h[:, c, :], start=(c == 0), stop=(c == 7))
                sg_ = stg[(ci // 4) % 2]
                if kind == "rw":
                    P.copy("dve", sg_[0:wd_, j % 4, :], pz[0:wd_, :])
                else:
                    P.act(sg_[0:wd_, j % 4, :], pz[0:wd_, :], AF.Sigmoid)
                if ci % 4 == 3:
                    j0 = j - 3
                    if kind == "rw":
                        if j == 15:
                            P.dma(TV(zrw[j0 * 128:j0 * 128 + 384, t0:t0 + G].rearrange("(c p) t -> p c t", p=128),
                                     "zrw_g%d" % gi), sg_[:, 0:3, :], "stg%d" % ((ci // 4) % 2))
                            P.dma(TV(zrw[1920:1952, t0:t0 + G], "zrw_g%d" % gi), sg_[0:32, 3, :], "stg%d" % ((ci // 4) % 2))
                        else:
                            P.dma(TV(zrw[j0 * 128:j0 * 128 + 512, t0:t0 + G].rearrange("(c p) t -> p c t", p=128),
                                     "zrw_g%d" % gi), sg_, "stg%d" % ((ci // 4) % 2))
                    else:
                        P.dma(TV(gatesT[j0 * 128:j0 * 128 + 512, t0:t0 + G].rearrange("(c p) t -> p c t", p=128),
                                 "gates_g%d" % gi), sg_, "stg%d" % ((ci // 4) % 2))
            gi += 1
            t0 += G
    P.barrier()


GN_EPS = 64e-5


def rwkv_phase(P, k, seqs, zrw, y0_d, yrwT_d, w):
    A = k.arena
    A.reset()
    BLK = 512
    NC = BLK // 64
    ps = k.ps
    W2 = A.alloc("W2", [128, 512])
    A2 = A.alloc("A2", [128, 512])
    G2a = A.alloc("G2a", [128, 512])
    G2b = A.alloc("G2b", [32, 512])
    cw = A.alloc("cw", [128, 16, 3], F32)
    pc = A.alloc("pc", [128, 32], F32)
    nw0 = A.alloc("nw0", [128, 8], F32)
    kc = A.alloc("kc", [128, 8], F32)
    rmask = A.alloc("rmask", [128, BLK], F32)
    idst = A.alloc("idst", [128, 64], F32)
    bones = A.alloc("bones", [128, 128])
    m1 = A.alloc("m1", [64, 2, 128], F32)
    m2 = A.alloc("m2", [64, 2, 64], F32)
    lnw = A.alloc("lnw", [64, 512], F32)
    lnb = A.alloc("lnb", [64, 512], F32)
    for dst, src, nm in ((W2, w["w2_rw"], "w2"), (A2, w["a2_rw"], "a2"), (G2a, w["g2_rw"][0:128, :], "g2a"),
                         (G2b, w["g2_rw"][128:160, :], "g2b")):
        P.dma(dst, TV(src, nm), "wload", eng="pool")
    P.dma(cw, TV(w["conv_c"], "conv_c"), "misc")
    P.dma(pc, TV(w["rw_pc"], "rw_pc"), "misc")
    P.dma(rmask, TV(w["rmask"], "rmask"), "misc")
    P.dma(idst, TV(w["idst"], "idst"), "misc")
    P.dma(bones, TV(w["bones"], "bones"), "misc")
    P.dma(m1, TV(w["m1"], "m1"), "misc")
    P.dma(m2, TV(w["m2"], "m2"), "misc")
    P.dma(lnw, TV(w["ln_w"].partition_broadcast(64), "ln_w"), "misc")
    P.dma(lnb, TV(w["ln_b"].partition_broadcast(64), "ln_b"), "misc")
    P.ts("dve", nw0, pc[:, 0:8], -1.0, None, op0=ALU.mult)
    P.memset("pool", kc[:, 0:1], 1.0)
    P.memset("pool", kc[:, 1:2], -0.5)
    P.memset("pool", kc[:, 2:3], GN_EPS)
    zin = [A.alloc("zin0", [128, 16, BLK + 2])] * 2
    zc = A.alloc("zc", [128, 16, BLK], F32)
    tw = A.alloc("tw", [128, BLK])
    xab = A.alloc("xab", [128, BLK])
    sgg = A.alloc("sgg", [128, BLK])
    sgg2 = A.alloc("sgg2", [32, BLK])
    f = {n: A.alloc(n, [128, BLK], F32) for n in ("ew", "a", "kk", "kd", "beta", "CE", "d1", "ex1", "ex2", "tmp")}
    f["e1"] = f["tmp"]
    f["t1"] = f["tmp"]
    f["sp"] = f["d1"]
    f["kkr"] = f["kk"]
    f["ssk"] = f["a"]
    f["ea"] = f["ex1"]
    f["em"] = f["ex1"]
    f["ep"] = f["ex2"]
    f["eh"] = f["ex2"]
    sqk = A.alloc("sqk", [128, BLK])
    bsb = sqk
    TE = A.alloc("TE", [128, NC], F32)
    wtot = A.alloc("wtot", [128, NC], F32)
    AR = [A.alloc("AR%d" % j, [128, NC, 2, 64]) for j in range(4)]
    BK = [A.alloc("BK%d" % j, [128, NC, 2, 64]) for j in range(4)]
    BKh = [A.alloc("BKh%d" % j, [128, NC, 2, 64]) for j in range(4)]
    VT = [A.alloc("VT%d" % j, [128, BLK + 64]) for j in range(4)]
    DW = [A.alloc("DW%d" % j, [128, NC, 64]) for j in range(4)]
    gT = A.alloc("gT", [128, 4, BLK])
    cb = A.alloc("cb", [128, 4, BLK])
    ynT = A.alloc("ynT", [128, 4, BLK])
    yo = A.alloc("yo", [128, 4, BLK])
    ATb = A.alloc("ATb", [64, 8, 128])
    ATk = A.alloc("ATk", [64, 8, 128])
    Nm = A.alloc("Nm", [64, 8, 64])
    Atm = A.alloc("Atm", [64, 8, 64])
    Bhtm = A.alloc("Bhtm", [64, 8, 64])
    Khtm = A.alloc("Khtm", [64, 8, 64])
    Vtm = A.alloc("Vtm", [64, 8, 64])
    X = A.alloc("X", [64, 8, 128])
    QQ = [A.alloc("QQ%d" % i, [64, 8, 128]) for i in range(2)]
    GE = A.alloc("GE", [64, 8, 128])
    Tst = A.alloc("Tst", [64, 8, 64])
    GEadd = A.alloc("GEadd", [64, NC, 8, 128])
    y0s = A.alloc("y0s", [64, NC, 512], F32)
    yt = A.alloc("yt", [64, 512], F32)
    yc = A.alloc("yc", [64, 512], F32)
    ysq = yt
    ynb = A.alloc("ynb", [64, 512])
    gst = A.alloc("gst", [64, 64], F32)
    for j in range(4):
        P.memset("pool", VT[j][:, BLK:BLK + 64], 0.0)
    ZA = k.psbig[0]
    ZB = k.psbig[1]
    ZC = k.psbig[2]
    za = lambda Z: Z[0:64, :].re("p (h c) -> p h c", h=8)
    idb = k.ident
    ps = [ZA[:, 0:512], ZA[:, 512:1024], ZB[:, 0:512], ZB[:, 512:1024]]

    def prep_pair(j, d, c, want_bonus_dirs, want_g):
        r_j, k_j, v_j = zc[:, j, :], zc[:, 4 + j, :], zc[:, 8 + j, :]
        hs = slice(64 * d, 64 * d + 64)
        P.mm(ps[0], W2[hs, j * 128:(j + 1) * 128], tw[hs, :])
        P.act(f["e1"], ps[0], AF.Exp, scale=-1.0, bias=nw0[:, d * 4 + j:d * 4 + j + 1])
        P.act(f["sp"], f["e1"], AF.Ln, bias=kc[:, 0:1])
        P.act(f["ew"], f["sp"], AF.Exp, scale=-1.0, bias=kc[:, 1:2])
        P.ts("dve", f["kkr"], k_j, pc[:, 16 + j:17 + j], None, op0=ALU.mult)
        P.act(sqk, f["kkr"], AF.Square)
        P.mm(ps[2], bones, sqk)
        P.ts("dve", f["ssk"], ps[2], 1e-24, None, op0=ALU.max)
        P.act(f["ssk"], f["ssk"], AF.Ln)
        P.act(f["ssk"], f["ssk"], AF.Exp, scale=-0.5)
        P.tt("dve", f["kk"], f["kkr"], f["ssk"], ALU.mult)
        first = True
        for dd in want_bonus_dirs:
            hs2 = slice(64 * dd, 64 * dd + 64)
            P.mm(ps[1], A2[hs2, j * 128:(j + 1) * 128], xab[hs2, :])
            P.act(f["a"], ps[1], AF.Sigmoid, bias=pc[:, 8 + dd * 4 + j:9 + dd * 4 + j])
            P.ts("dve", f["t1"], f["a"], -1.0, pc[:, 20 + j:21 + j], op0=ALU.add, op1=ALU.mult)
            P.stt("dve", f["kd"], f["t1"], 1.0, k_j, ALU.add, ALU.mult)
            if len(want_bonus_dirs) > 1:
                P.stt("dve", bsb, r_j, pc[:, 24 + dd * 4 + j:25 + dd * 4 + j], f["kd"], ALU.mult, ALU.mult)
                P.mm(ps[3], bones, bsb, start=first, stop=(dd == want_bonus_dirs[-1]))
                first = False
        if len(want_bonus_dirs) > 1:
            P.tt("dve", cb[:, j, :], ps[3], v_j, ALU.mult)
        if want_g:
            P.mm(ps[1], G2a[:, j * 128:(j + 1) * 128], sgg, start=True, stop=False)
            P.mm(ps[1], G2b[:, j * 128:(j + 1) * 128], sgg2, start=False, stop=True)
            P.copy("act", gT[:, j, :], ps[1])
        P.tt("dve", f["beta"], f["kk"], f["a"], ALU.mult)
        P.scan(f["CE"], rmask, f["ew"], 0.0, ALU.mult, ALU.add)
        ce3 = f["CE"].re("p (c t) -> p c t", c=NC)
        P.copy("dve", TE, ce3[:, :, 63])
        if d == 1:
            P.tt("dve", f["tmp"].re("p (c t) -> p c t", c=NC), TE.us(2).bc([128, NC, 64]), ce3, ALU.subtract)
            P.tt("dve", f["CE"], f["tmp"], f["ew"], ALU.add)
        P.act(wtot, TE, AF.Exp, scale=-1.0)
        v3 = lambda t_: t_.re("p (c t) -> p c t", c=NC)
        P.tt("dve", f["d1"], f["ew"], f["CE"], ALU.subtract)
        P.act(f["ea"], f["d1"], AF.Exp)
        P.act(f["ep"], f["CE"], AF.Exp)
        P.stt("dve", AR[j][:, :, 0, :], v3(f["kk"]), -1.0, v3(f["ea"]), ALU.mult, ALU.mult)
        P.act(f["em"], f["CE"], AF.Exp, scale=-1.0)
        P.tt("dve", BK[j][:, :, 0, :], v3(f["beta"]), v3(f["ep"]), ALU.mult)
        P.tt("dve", BK[j][:, :, 1, :], v3(f["kd"]), v3(f["ep"]), ALU.mult)
        P.tt("dve", f["tmp"].re("p (c t) -> p c t", c=NC), ce3, TE.us(2).bc([128, NC, 64]), ALU.subtract)
        P.act(f["eh"], f["tmp"], AF.Exp)
        P.tt("dve", AR[j][:, :, 1, :], v3(r_j), v3(f["em"]), ALU.mult)
        P.tt("dve", BKh[j][:, :, 0, :], v3(f["beta"]), v3(f["eh"]), ALU.mult)
        P.tt("dve", BKh[j][:, :, 1, :], v3(f["kd"]), v3(f["eh"]), ALU.mult)
        P.copy("act", VT[j][:, 0:BLK], v_j)
        P.tt("dve", DW[j], idst.us(1).bc([128, NC, 64]), wtot.us(2).bc([128, NC, 64]), ALU.mult)
        P.copy("act", GEadd[:, :, j, 0:64], DW[j][0:64, :, :])
        P.copy("act", GEadd[:, :, j, 64:128], AR[j][0:64, :, 1, :])
        P.mm(ps[2][0:64, :], idb[64:128, 64:128], DW[j][64:128, :, :].re("p c k -> p (c k)"))
        P.mm(ps[3][0:64, :], idb[64:128, 64:128], AR[j][64:128, :, 1, :])
        P.copy("act", GEadd[:, :, 4 + j, 0:64], ps[2][0:64, :].re("p (c k) -> p c k", c=NC))
        P.copy("act", GEadd[:, :, 4 + j, 64:128], ps[3][0:64, :].re("p (c k) -> p c k", c=NC))

    NP = 2
    SP = 8 // NP
    for Z_ in (ZA, ZB, ZC):
        P.alias[Z_.name] = [Z_.name + "/%d" % p_ for p_ in range(NP)]

    def hv(t_, p):
        return TV(t_.ap[:, SP * p:SP * p + SP], t_.name + "/%d" % p)

    def zh(Z, p):
        return TV(Z.ap[0:64, p * SP * 128:(p + 1) * SP * 128].rearrange("p (h c) -> p h c", h=SP), Z.name + "/%d" % p)

    def unit_chunk(c, d, t_glob, last_sweep):
        PARTS = range(NP)
        m1b = m1[:, d, :].us(1).bc([64, SP, 128])
        m2b = m2[:, d, :].us(1).bc([64, SP, 64])
        pa, pb = k.psb[0], k.psb[1]
        for j in range(4):
            P.tr(pa[0:64, j * 128:(j + 1) * 128], AR[j][:, c, 0, :], idb)
            P.tr(pa[0:64, 512 + j * 128:512 + (j + 1) * 128], BKh[j][:, c, 0, :], idb)
            P.tr(pb[0:64, j * 128:(j + 1) * 128], BKh[j][:, c, 1, :], idb)
            P.tr(pb[:, 512 + j * 128:512 + (j + 1) * 128], VT[j][:, c * 64:c * 64 + 128], idb)
        so = lambda t_: t_.re("p (e j) c -> p j e c", e=2)
        si = lambda t_: t_.re("p (j e c) -> p j e c", j=4, e=2)
        P.copy("act", so(Atm), si(pa[0:64, 0:512]))
        P.copy("act", so(Bhtm), si(pa[0:64, 512:1024]))
        P.copy("act", so(Khtm), si(pb[0:64, 0:512]))
        P.copy("act", so(Vtm), si(pb[0:64, 512:1024]))
        for p in PARTS:
            za_, zb_, zc_ = zh(ZA, p), zh(ZB, p), zh(ZC, p)
            for jj in range(SP):
                sl = SP * p + jj
                e, j = sl // 4, sl % 4
                hs = slice(64 * e, 64 * e + 64)
                ar = AR[j][hs, c, :, :].re("p a t -> p (a t)")
                P.mm(za_[:, jj, :], BK[j][hs, c, 0, :], ar)
                P.mm(zb_[:, jj, :], BK[j][hs, c, 1, :], ar)
                P.mm(zc_[:, jj, 0:64], AR[j][hs, c, 0, :], BK[j][hs, c, 0, :])
        for p in PARTS:
            P.tt("dve", hv(ATb, p), zh(ZA, p), m1b, ALU.mult)
            P.tt("dve", hv(ATk, p), zh(ZB, p), m1b, ALU.mult)
            P.tt("dve", hv(Nm, p), zh(ZC, p)[:, :, 0:64], m2b, ALU.mult)
        for p in PARTS:
            zc_ = zh(ZC, p)
            for jj in range(SP):
                P.mm(zc_[:, jj, 64:128], hv(ATk, p)[:, jj, 0:64], Vtm[:, SP * p + jj, :])
        for p in PARTS:
            P.copy("act", hv(X, p)[:, :, 0:64], Atm[:, SP * p:SP * p + SP, :])
            P.copy("dve", hv(X, p)[:, :, 64:128], zh(ZC, p)[:, :, 64:128])
        for it in range(6):
            Z = ZA if it % 2 == 0 else ZB
            for p in PARTS:
                z_, zc_, x_ = zh(Z, p), zh(ZC, p), hv(X, p)
                qq_prev = hv(QQ[(it - 1) % 2], p)
                for jj in range(SP):
                    qt = hv(ATb, p)[:, jj, 0:64] if it == 0 else qq_prev[:, jj, 64:128]
                    P.mm(z_[:, jj, :], qt, x_[:, jj, :])
                if it < 5:
                    for jj in range(SP):
                        q = hv(Nm, p)[:, jj, :] if it == 0 else qq_prev[:, jj, 0:64]
                        qt = hv(ATb, p)[:, jj, 0:64] if it == 0 else qq_prev[:, jj, 64:128]
                        if it < 4:
                            P.mm(zc_[:, jj, 0:64], qt, q)
                        P.mm(zc_[:, jj, 64:128], q, qt)
            for p in PARTS:
                if it < 4:
                    P.copy("act", hv(QQ[it % 2], p), zh(ZC, p))
                elif it == 4:
                    P.copy("act", hv(QQ[it % 2], p)[:, :, 64:128], zh(ZC, p)[:, :, 64:128])
                P.tt("dve", hv(X, p), hv(X, p), zh(Z, p), ALU.add)
        for p in PARTS:
            za_, x_ = zh(ZA, p), hv(X, p)
            for jj in range(SP):
                h = SP * p + jj
                P.mm(za_[:, jj, 0:64], x_[:, jj, 0:64], Bhtm[:, h, :])
                P.mm(za_[:, jj, 64:128], x_[:, jj, 0:64], hv(ATb, p)[:, jj, 64:128])
        for p in PARTS:
            P.tt("dve", hv(GE, p), zh(ZA, p), GEadd[:, c, SP * p:SP * p + SP, :], ALU.add)
        for p in PARTS:
            zb_, x_, ge_, ts_ = zh(ZB, p), hv(X, p), hv(GE, p), hv(Tst, p)
            for jj in range(SP):
                h = SP * p + jj
                P.mm(zb_[:, jj, 0:64], Bhtm[:, h, :], x_[:, jj, 64:128], start=True, stop=False)
                P.mm(zb_[:, jj, 0:64], Khtm[:, h, :], Vtm[:, h, :], start=False, stop=False)
                P.mm(zb_[:, jj, 0:64], ge_[:, jj, 0:64], ts_[:, jj, :], start=False, stop=True)
                P.mm(zb_[:, jj, 64:128], hv(ATb, p)[:, jj, 64:128], x_[:, jj, 64:128], start=True, stop=False)
                P.mm(zb_[:, jj, 64:128], hv(ATk, p)[:, jj, 64:128], Vtm[:, h, :], start=False, stop=False)
                P.mm(zb_[:, jj, 64:128], ge_[:, jj, 64:128], ts_[:, jj, :], start=False, stop=True)
        for p in PARTS:
            w_ = SP * 64
            e, j0 = (SP * p) // 4, (SP * p) % 4
            P.copy("dve", hv(Tst, p), zh(ZB, p)[:, :, 0:64])
            if not last_sweep:
                P.copy("dve", y0s[:, c, p * w_:(p + 1) * w_].re("p (j v) -> p j v", j=SP), zh(ZB, p)[:, :, 64:128])
            else:
                P.tt("dve", yt.re("p (j e v) -> p e j v", j=4, e=2)[:, e, j0:j0 + SP, :], zh(ZB, p)[:, :, 64:128],
                     y0s[:, c, p * w_:(p + 1) * w_].re("p (j v) -> p j v", j=SP), ALU.add)
        if last_sweep:
            y3 = lambda t_: t_.re("p (h v) -> p h v", h=8)
            P.reduce("dve", gst[:, 0:8], y3(yt))
            P.ts("dve", gst[:, 8:16], gst[:, 0:8], 1.0 / 64, None, op0=ALU.mult)
            P.tt("dve", y3(yc), y3(yt), gst[:, 8:16].us(2).bc([64, 8, 64]), ALU.subtract)
            P.act(ysq, yc, AF.Square)
            P.reduce("dve", gst[:, 16:24], y3(ysq))
            P.act(gst[:, 24:32], gst[:, 16:24], AF.Ln, scale=1.0 / 64, bias=kc[0:64, 2:3])
            P.act(gst[:, 32:40], gst[:, 24:32], AF.Exp, scale=-0.5)
            P.tt("dve", y3(yc), y3(yc), gst[:, 32:40].us(2).bc([64, 8, 64]), ALU.mult)
            P.tt("dve", yc, yc, lnw, ALU.mult)
            P.tt("dve", ynb, yc, lnb, ALU.add)
            for j in range(4):
                P.tr(k.psb[0][:, j * 64:(j + 1) * 64], ynb[:, j * 128:(j + 1) * 128], idb[0:64, 0:64])
            P.copy("act", ynT[:, :, c * 64:(c + 1) * 64], k.psb[0][:, 0:256].re("p (j t) -> p j t", j=4))

    blocks = []
    t_seq = 0
    for s, S in enumerate(seqs):
        nb = S // BLK
        for d in range(2):
            if d == 1 and k.rw_stop == "e_y0":
                continue
            order = range(nb) if d == 0 else range(nb - 1, -1, -1)
            for i_, b in enumerate(order):
                blocks.append(dict(s=s, d=d, b=b, nb=nb, t0=t_seq + b * BLK, first=(i_ == 0)))
        t_seq += S

    def load_conv(blk, conv_now):
        b, nb, t0 = blk["b"], blk["nb"], blk["t0"]
        z = zin[0]
        lo = 1 if b == 0 else 0
        hi = BLK + 1 if b == nb - 1 else BLK + 2
        if b == 0:
            P.memset("pool", z[:, :, 0:1], 0.0)
        if b == nb - 1:
            P.memset("pool", z[:, :, BLK + 1:BLK + 2], 0.0)
        P.dma(z[:, 0:15, lo:hi], TV(zrw[0:1920, t0 - 1 + lo:t0 - 1 + hi].rearrange("(c p) t -> p c t", p=128),
                                     "zrw_g%d" % (t0 // 512)), "xin0")
        P.dma(z[0:32, 15, lo:hi], TV(zrw[1920:1952, t0 - 1 + lo:t0 - 1 + hi], "zrw_g%d" % (t0 // 512)), "xin0")
        if conv_now:
            for c in range(16):
                conv_chunk(c)

    def conv_chunk(c):
        z = zin[0]
        np_ = 128 if c < 15 else 32
        o = zc[0:np_, c, :]
        P.act(o, z[0:np_, c, 0:BLK], AF.Copy, scale=cw[0:np_, c, 0:1])
        P.stt("dve", o, z[0:np_, c, 1:BLK + 1], cw[0:np_, c, 1:2], o, ALU.mult, ALU.add)
        P.stt("dve", o, z[0:np_, c, 2:BLK + 2], cw[0:np_, c, 2:3], o, ALU.mult, ALU.add)

    if blocks:
        load_conv(blocks[0], True)
    for bi, blk in enumerate(blocks):
        d, t0 = blk["d"], blk["t0"]
        if blk["first"]:
            for p_ in range(NP):
                P.memset("pool", hv(Tst, p_), 0.0)
        if d == 1:
            P.dma(y0s, TV(y0_d[t0:t0 + BLK, :].rearrange("(c p) v -> p c v", p=64), "y0_%d" % (t0 // 512)), "y0in")
        if k.rw_stop in ("load", "conv"):
            continue
        P.act(tw, zc[:, 12, :], AF.Tanh)
        P.copy("act", xab, zc[:, 13, :])
        if d == 1:
            P.act(sgg, zc[:, 14, :], AF.Sigmoid)
            P.act(sgg2, zc[0:32, 15, :], AF.Sigmoid)
        for j in range(4):
            prep_pair(j, d, None, [0, 1] if d == 1 else [0], d == 1)
        have_next = bi + 1 < len(blocks)
        if have_next:
            load_conv(blocks[bi + 1], False)
        if k.rw_stop == "prep":
            continue
        corder = range(NC) if d == 0 else range(NC - 1, -1, -1)
        for ci_, c in enumerate(corder):
            unit_chunk(c, d, t0 + c * 64, d == 1)
            if have_next:
                conv_chunk(2 * ci_)
                conv_chunk(2 * ci_ + 1)
        if d == 0:
            P.dma(TV(y0_d[t0:t0 + BLK, :].rearrange("(c p) v -> p c v", p=64), "y0_%d" % (t0 // 512)), y0s, "y0out")
        else:
            P.tt("dve", ynT, ynT, cb, ALU.add)
            P.tt("dve", yo, ynT, gT, ALU.mult)
            P.dma(TV(yrwT_d[:, t0:t0 + BLK].rearrange("(j p) t -> p j t", p=128), "yrwT_%d" % (t0 // 512)), yo, "yrwout")
    P.barrier()


def na_phase(P, k, seqs, qT, kT, vna, ynaT_d, w):
    A = k.arena
    A.reset()
    Bt = A.alloc("Bt", [128, 8, 16, 64], F32)
    idq = A.alloc("idq", [64, 64], F32)
    qg = [A.alloc("qg%d" % i, [128, 4, 512]) for i in range(2)]
    kw = [A.alloc("kw%d" % i, [128, 4, 1024]) for i in range(2)]
    vE = [A.alloc("vE%d" % i, [128, 8, 512]) for i in range(2)]
    vO = [A.alloc("vO%d" % i, [128, 8, 512]) for i in range(2)]
    sb = [A.alloc("sb%d" % i, [128, 1024], F32) for i in range(2)]
    PT = [A.alloc("PT%d" % i, [128, 4, 4, 64]) for i in range(2)]
    dtmp = A.alloc("dtmp", [64, 512], F32)
    den = A.alloc("den", [64, 16], F32)
    ynb = A.alloc("ynb", [64, 512])
    yst = [A.alloc("yst%d" % i, [128, 4, 512]) for i in range(2)]
    P.dma(Bt, TV(w["na_bias"], "na_bias"), "misc")
    P.dma(idq, TV(w["idst"][0:64, :], "idst"), "misc")
    ST = [k.psbig[0], k.psbig[1]]
    OUT = k.ps[4]
    DEN = k.ps[5]
    t_seq = 0
    gi = 0
    for s, S in enumerate(seqs):
        R = S // 64
        for g in range(R // 8):
            b = gi % 2
            rk0 = min(max(8 * g - 4, 0), R - 16)
            t0 = t_seq + g * 512
            tk = t_seq + rk0 * 64
            P.dma(qg[b], TV(qT[:, t0:t0 + 512].rearrange("(j p) t -> p j t", p=128), "qT_g%d" % (t0 // 512)), "xin%d" % b)
            for kk_ in range(2):
                P.dma(kw[b][:, :, kk_ * 512:(kk_ + 1) * 512],
                      TV(kT[:, tk + kk_ * 512:tk + (kk_ + 1) * 512].rearrange("(j p) t -> p j t", p=128),
                         "kT_g%d" % ((tk + kk_ * 512) // 512)), "xin%d" % b)
            P.dma(vE[b], TV(vna[tk:tk + 1024, :].rearrange("(i p) c -> p i c", p=128), "vna_w%d" % (tk // 512)), "vin%d" % b)
            P.dma(vO[b][:, 0:7, :], TV(vna[tk + 64:tk + 64 + 896, :].rearrange("(i p) c -> p i c", p=128),
                                       "vna_w%d" % (tk // 512)), "vin%d" % b)
            ys = yst[b]
            for lr in range(8):
                r = 8 * g + lr
                r0 = min(max(r - 4, 0), R - 8)
                o = r0 - rk0
                pt = PT
                for j in range(4):
                    koff = (o + 2 * j) * 64
                    for jh in range(4):
                        for e in range(2):
                            hs = slice(64 * e, 64 * e + 64)
                            P.mm(ST[e][:, (j * 4 + jh) * 64:(j * 4 + jh + 1) * 64], kw[b][hs, jh, koff:koff + 128],
                                 qg[b][hs, jh, lr * 64:(lr + 1) * 64])
                for e in range(2):
                    for j in range(4):
                        qd = r - (r0 + 2 * j) + 7
                        P.tt("dve", sb[e][:, j * 256:(j + 1) * 256].re("p (a q) -> p a q", a=4),
                             ST[e][:, j * 256:(j + 1) * 256].re("p (a q) -> p a q", a=4),
                             Bt.re("p (a e) d q -> p a e d q", e=2)[:, :, e, qd, :], ALU.add)
                    P.act(pt[e].re("p j a q -> p (j a q)"), sb[e], AF.Exp)
                for h in range(8):
                    jh, e = h // 2, h % 2
                    for j in range(4):
                        ci = o + 2 * j
                        vt = vE[b][:, ci // 2, :] if ci % 2 == 0 else vO[b][:, (ci - 1) // 2, :]
                        P.mm(OUT[0:64, h * 64:(h + 1) * 64], pt[e][:, j, jh, :], vt[:, h * 64:(h + 1) * 64],
                             start=(j == 0), stop=(j == 3))
                for e in range(2):
                    for j in range(4):
                        P.mm(DEN[0:64, e * 256:(e + 1) * 256], k.ones[:, 0:64], pt[e][:, j, :, :].re("p a q -> p (a q)"),
                             start=(j == 0), stop=(j == 3))
                P.tt("dve", dtmp.re("p (a q) -> p a q", a=8), DEN[0:64, :].re("p (a q) -> p a q", a=8),
                     idq.us(1).bc([64, 8, 64]), ALU.mult)
                P.reduce("dve", den[:, 0:8], dtmp.re("p (a q) -> p a q", a=8))
                P.recip(den[:, 8:16], den[:, 0:8])
                P.tt("dve", ynb.re("p (a e v) -> p e a v", a=4, e=2), OUT[0:64, :].re("p (a e v) -> p e a v", a=4, e=2),
                     den[:, 8:16].re("p (e a) -> p e a", e=2).us(3).bc([64, 2, 4, 64]), ALU.mult)
                for jh in range(4):
                    P.tr(k.psb[0][:, jh * 64:(jh + 1) * 64], ynb[:, jh * 128:(jh + 1) * 128], k.ident[0:64, 0:64])
                P.copy("act", ys[:, :, lr * 64:(lr + 1) * 64], k.psb[0][:, 0:256].re("p (j t) -> p j t", j=4))
            P.dma(TV(ynaT_d[:, t0:t0 + 512].rearrange("(j p) t -> p j t", p=128), "ynaT_%d" % (t0 // 512)), ys,
                  "ynaout%d" % b)
            gi += 1
        t_seq += S
    P.barrier()


def merge_phase(P, k, T, x1, ynaT, yrwT, ymT, gatesT, w):
    A = k.arena
    A.reset()
    G = 512
    WO = [A.alloc("WO%d" % i, [128, 4, D]) for i in range(3)]
    WOUT = A.alloc("WOUT", [128, 8, D])
    yb = [[A.alloc("yb%d_%d" % (i, b), [128, 4, G]) for b in range(3)] for i in range(2)]
    gt = [A.alloc("gt%d" % i, [128, 24, G]) for i in range(2)]
    xg = [A.alloc("xg%d" % i, [128, 4, D], F32) for i in range(2)]
    m0 = [A.alloc("m0_%d" % i, [128, G], F32) for i in range(2)]
    tm = [A.alloc("tm_%d" % i, [128, G], F32) for i in range(2)]
    mb = A.alloc("mb", [128, 8, G])
    for i, nm in enumerate(("w_o_na", "w_o_rw", "w_o_mem")):
        load_weight_cast(P, WO[i], w[nm], 4, nm, "wload")
    load_weight_cast(P, WOUT, w["w_out"], 8, "w_out", "wload")
    srcs = (ynaT, yrwT, ymT)
    ps = k.ps
    def load_g(g_):
        b_ = g_ % 2
        t_ = g_ * G
        for i in range(3):
            P.dma(yb[b_][i], TV(srcs[i][:, t_:t_ + G].rearrange("(j p) t -> p j t", p=128), "ysrc%d_%d" % (i, g_)),
                  "xin%d" % b_)
        P.dma(gt[b_], TV(gatesT[:, t_:t_ + G].rearrange("(c p) t -> p c t", p=128), "gates_g%d" % g_), "gin%d" % b_)
        P.dma(xg[b_], TV(x1[t_:t_ + G, :].rearrange("(i p) d -> p i d", p=128), "x1_g%d" % g_), "x1in%d" % b_)

    load_g(0)
    for g in range(T // G):
        b = g % 2
        t0 = g * G
        if g + 1 < T // G:
            load_g(g + 1)
        for oc in range(8):
            for i in range(3):
                pz = ps[(oc * 3 + i) % 4]
                for kc in range(4):
                    P.mm(pz, WO[i][:, kc, oc * 128:(oc + 1) * 128], yb[b][i][:, kc, :], start=(kc == 0), stop=(kc == 3))
                if i == 0:
                    P.tt("dve", m0[oc % 2], pz, gt[b][:, oc, :], ALU.mult)
                else:
                    P.tt("dve", tm[i % 2], pz, gt[b][:, i * 8 + oc, :], ALU.mult)
                    if i == 1:
                        P.tt("pool", m0[oc % 2], m0[oc % 2], tm[1], ALU.add)
                    else:
                        P.tt("pool", mb[:, oc, :], m0[oc % 2], tm[0], ALU.add)
        for i in range(4):
            for n in range(2):
                pd = ps[4 + (2 * i + n) % 2]
                for kc in range(8):
                    P.mm(pd, mb[:, kc, i * 128:(i + 1) * 128], WOUT[:, kc, n * 512:(n + 1) * 512],
                         start=(kc == 0), stop=(kc == 7))
                P.tt("dve", xg[b][:, i, n * 512:(n + 1) * 512], pd, xg[b][:, i, n * 512:(n + 1) * 512], ALU.add)
        P.dma(TV(x1[t0:t0 + G, :].rearrange("(i p) d -> p i d", p=128), "x1_g%d" % g), xg[b], "x1out%d" % b)
    P.barrier()


def build(seqs, phases="ABCNDE", dbg=(), rw_stop=None):
    T = sum(seqs)
    nc = bass.Bass("TRN2", target_bir_lowering=False)
    k = K()
    k.nc = nc
    k.rw_stop = rw_stop

    def din(name, shape, dt=F32):
        return nc.dram_tensor(name, list(shape), dt, kind="ExternalInput").ap()

    def dscr(name, shape, dt):
        kind = "ExternalOutput" if name in dbg else "Internal"
        return nc.dram_tensor(name, list(shape), dt, kind=kind).ap()

    x = din("x", [T, D])
    y = nc.dram_tensor("y", [T, D], F32, kind="ExternalOutput").ap()
    ident_d = din("ident", [128, 128], BF16)
    w = {}
    for nm, shp in (("w_ffn1_gate", [D, DFF]), ("w_ffn1_up", [D, DFF]), ("w_ffn1_down", [DFF, D]),
                    ("w_ffn2_gate", [D, DFF]), ("w_ffn2_up", [D, DFF]), ("w_ffn2_down", [DFF, D]),
                    ("g_ffn1_c", [128, 8]), ("g_ffn2_c", [128, 8]), ("w_in", [D, NIN]), ("g_mix_c", [128, 8]),
                    ("gvec_q", [1, 1536]), ("mem", [256 * len(seqs), D]), ("w_mem_kv", [D, D]),
                    ("g_mem_c", [128, 8]), ("gvec_km", [1, 512]),
                    ("w2_rw", [128, 512]), ("a2_rw", [128, 512]), ("g2_rw", [160, 512]), ("conv_c", [128, 16, 3]),
                    ("rw_pc", [128, 32]), ("rmask", [128, 512]), ("idst", [128, 64]),
                    ("m1", [64, 2, 128]), ("m2", [64, 2, 64]), ("ln_w", [1, 512]), ("ln_b", [1, 512])):
        w[nm] = din(nm, shp)
    w["bones"] = din("bones", [128, 128], BF16)
    for nm, shp in (("na_bias", [128, 8, 16, 64]), ("w_o_na", [512, D]), ("w_o_rw", [512, D]), ("w_o_mem", [512, D]),
                    ("w_out", [D, D])):
        w[nm] = din(nm, shp)
    ynaT = dscr("ynaT", [512, T], BF16)
    y0_d = dscr("y0", [T, 512], F32)
    yrwT = dscr("yrwT", [512, T], BF16)
    x1 = dscr("x1", [T, D], F32)
    h2T = dscr("h2T", [D, T], BF16)
    qT = dscr("qT", [512, T], BF16)
    kT = dscr("kT", [512, T], BF16)
    vna = dscr("vna", [T, 512], BF16)
    zrw = dscr("zrw", [NRW, T], BF16)
    ymT = dscr("ymT", [512, T], BF16)
    gatesT = dscr("gatesT", [3072, T], BF16)

    with ExitStack() as st:
        arena_t = st.enter_context(nc.sbuf_tensor("arena", [128, 98000], BF16))
        k.arena = Arena(arena_t[:, :], 196000)
        cst = st.enter_context(nc.sbuf_tensor("cst", [128, 8192], BF16))
        k.ident = TV(cst[:, 0:128], "ident")
        barscr = TV(cst[:, 128:192], "barscr")
        k.ones = TV(cst[:, 256:384], "ones")
        k.kmT = [TV(cst[:, 512 + i * 1024:512 + (i + 1) * 1024].rearrange("p (h m) -> p h m", h=4), "kmT%d" % i)
                 for i in range(2)]
        k.vm = [TV(cst[:, 2560 + i * 1024:2560 + (i + 1) * 1024].rearrange("p (c v) -> p c v", c=2), "vm%d" % i)
                for i in range(2)]
        big = [st.enter_context(nc.psum_tensor("psbig%d" % i, [128, 1024], F32)) for i in range(3)]
        k.psbig = [TV(big[i][:, :], "psbig%d" % i) for i in range(3)]
        k.ps = [TV(big[i // 2][:, (i % 2) * 512:(i % 2 + 1) * 512], "ps%d" % i) for i in range(6)]
        k.psb = [TV(st.enter_context(nc.psum_tensor("psb%d" % i, [128, 1024], BF16))[:, :], "psb%d" % i)
                 for i in range(2)]
        P = Prog(nc)
        bs = barscr.ap
        psbar = k.ps[5].ap
        idap = k.ident.ap
        P.bar_fns = {
            "pe": lambda e: e.matmul(psbar[0:64, 0:32], lhsT=idap[0:64, 0:64], rhs=idap[0:64, 0:32], start=True, stop=True),
            "act": lambda e: e.activation(out=bs[:, 0:8], in_=bs[:, 8:16], func=AF.Copy),
            "dve": lambda e: e.tensor_copy(out=bs[:, 16:24], in_=bs[:, 24:32]),
            "pool": lambda e: e.tensor_copy(out=bs[:, 32:40], in_=bs[:, 40:48]),
        }
        P.add("pool", lambda e: e.memset(bs, 0.0), writes=["bs_act", "bs_dve", "bs_pool"])
        P.memset("pool", k.ones, 1.0)
        P.dma(k.ident, TV(ident_d, "ident_d"), "misc")
        P.barrier()
        if "A" in phases:
            ffn_phase(P, k, "A", x, x1, T, w["w_ffn1_gate"], w["w_ffn1_up"], w["w_ffn1_down"], w["g_ffn1_c"],
                      h2T_dram=h2T)
        if "B" in phases:
            memkv_phase(P, k, seqs, w["mem"], w["w_mem_kv"], w["g_mem_c"], w["gvec_km"])
            proj_phase(P, k, seqs, h2T, w["w_in"], w["g_mix_c"], w["gvec_q"], qT, kT, vna, zrw, ymT, gatesT)
        if "C" in phases:
            rwkv_phase(P, k, seqs, zrw, y0_d, yrwT, w)
        if "N" in phases:
            na_phase(P, k, seqs, qT, kT, vna, ynaT, w)
        if "D" in phases:
            merge_phase(P, k, T, x1, ynaT, yrwT, ymT, gatesT, w)
        if "E" in phases:
            ffn_phase(P, k, "E", x1, y, T, w["w_ffn2_gate"], w["w_ffn2_up"], w["w_ffn2_down"], w["g_ffn2_c"])
        P.finalize(st)
        k.P = P
    return nc, k


def _gcol(g):
    return np.ascontiguousarray(np.asarray(g, np.float32).reshape(8, 128).T)


def make_inputs(d, b, seqdefs):
    f = lambda n: np.ascontiguousarray(np.asarray(d[n][0], np.float32))
    inp = {"x": np.ascontiguousarray(np.concatenate([np.asarray(d[xn][b, :S]) for xn, mn, S in seqdefs], 0)),
           "mem": np.ascontiguousarray(np.concatenate([np.asarray(d[mn][b]) for xn, mn, S in seqdefs], 0)),
           "ident": np.eye(128).astype(ml_dtypes.bfloat16)}
    for n in ("w_ffn1_gate", "w_ffn1_up", "w_ffn1_down", "w_ffn2_gate", "w_ffn2_up", "w_ffn2_down", "w_in",
              "w_mem_kv"):
        inp[n] = f(n)
    inp["g_ffn1_c"] = _gcol(d["g_ffn1"][0])
    inp["g_ffn2_c"] = _gcol(d["g_ffn2"][0])
    inp["g_mix_c"] = _gcol(d["g_mix"][0])
    inp["g_mem_c"] = _gcol(d["g_mem_norm"][0])
    inp["gvec_q"] = np.concatenate([np.tile(f("g_qn_na"), 8), np.tile(f("g_kn_na"), 8),
                                    np.tile(f("g_qn_mem"), 4)])[None, :].astype(np.float32)
    inp["gvec_km"] = np.tile(f("g_kn_mem"), 4)[None, :].astype(np.float32)
    inp["w2_rw"] = f("w2_rw").reshape(128, 512)
    inp["a2_rw"] = f("a2_rw").reshape(128, 512)
    inp["g2_rw"] = f("g2_rw")
    cv = np.zeros((3, 2048), np.float32)
    cv[:, :1952] = f("conv_rw")
    inp["conv_c"] = np.ascontiguousarray(cv.reshape(3, 16, 128).transpose(2, 1, 0))
    col = lambda v: np.asarray(v, np.float32).reshape(-1, 128).T
    inp["rw_pc"] = np.ascontiguousarray(np.concatenate(
        [col(f("w0_rw").reshape(-1)), col(f("a0_rw").reshape(-1)), col(f("k_k_rw")), col(f("k_a_rw")),
         col(f("r_k_rw").reshape(-1))], axis=1))
    rm = np.ones((128, 512), np.float32)
    rm[:, ::64] = 0.0
    inp["rmask"] = rm
    inp["idst"] = np.concatenate([np.eye(64), np.eye(64)], 0).astype(np.float32)
    bo = np.zeros((128, 128), np.float32)
    bo[:64, :64] = 1.0
    bo[64:, 64:] = 1.0
    inp["bones"] = bo.astype(ml_dtypes.bfloat16)
    idx = np.arange(64)
    m1 = np.zeros((64, 2, 128), np.float32)
    m1[:, 0, :64] = idx[:, None] < idx[None, :]
    m1[:, 0, 64:] = idx[:, None] <= idx[None, :]
    m1[:, 1, :64] = idx[:, None] > idx[None, :]
    m1[:, 1, 64:] = idx[:, None] >= idx[None, :]
    m2 = np.zeros((64, 2, 64), np.float32)
    m2[:, 0, :] = idx[None, :] < idx[:, None]
    m2[:, 1, :] = idx[None, :] > idx[:, None]
    inp["m1"] = m1
    inp["m2"] = m2
    for n in ("w_o_na", "w_o_rw", "w_o_mem", "w_out"):
        inp[n] = f(n)
    rpb = f("rpb_na")
    cols = np.arange(64)
    cs = np.clip(cols - 8, 0, 48)
    kc_, qc_ = np.meshgrid(cols, cols, indexing="ij")
    valid = (kc_ >= cs[None, :]) & (kc_ < cs[None, :] + 16)
    dc = np.clip(kc_ - qc_ + 15, 0, 30)
    tab = np.full((2, 64, 8, 16, 64), -30000.0, np.float32)
    for jj in range(2):
        for qd in range(16):
            dr = jj + 7 - qd
            if -7 <= dr <= 7:
                vals = rpb[:, dr + 7, :][:, dc]
                tab[jj, :, :, qd, :] = np.where(valid[:, None, :], vals.transpose(1, 0, 2), -30000.0)
    inp["na_bias"] = np.ascontiguousarray(tab.reshape(128, 8, 16, 64))
    inp["ln_w"] = f("ln_x_w_rw")[None, :]
    inp["ln_b"] = f("ln_x_b_rw")[None, :]
    return inp


SEQS = (4096, 8192)


def kernel(**inp):
    n = 8
    nc, _ = build(list(SEQS))
    in_maps = [make_inputs(inp, b, [("x_prompt", "mem_prompt", SEQS[0]), ("x_sample", "mem_sample", SEQS[1])])
               for b in range(n)]
    res = run_bass_kernel_spmd(nc, in_maps, core_ids=list(range(n)))
    ys = [np.asarray(r["y"], np.float32) for r in res.results]
    y_prompt = np.stack([y[:SEQS[0]] for y in ys], 0)
    y_sample = np.stack([y[SEQS[0]:] for y in ys], 0)
    return (y_prompt, y_sample)
```
